# Optimizing a Trainium2 kernel written in Bass

```python
import math
import jax
import jax.numpy as jnp
from jax import lax
import numpy as np

D_MODEL = 1024
BATCH = 16
SEQ = 256
DEPTH = 4
DEC_BATCH = 2
DEC_SEQ = 1024
PAST_LEN = 512

GRID_W = 64
HEAD_DIM = 64
MIX_WIDTH = D_MODEL
A_WIDTH = MIX_WIDTH // 2
A_HEADS = A_WIDTH // HEAD_DIM
B_WIDTH = MIX_WIDTH - A_WIDTH
POOL_WINDOWS = (2, 4, 8, 16)
B_GROUPS = len(POOL_WINDOWS)
B_GROUP_DIM = B_WIDTH // B_GROUPS
NA_ROWS = 8
NA_COLS = 16
NA_QCOLS = 16
NA_SPAN = NA_QCOLS + NA_COLS
C_HEADS = MIX_WIDTH // HEAD_DIM
C_KV_HEADS = 4
C_Q_WIDTH = C_HEADS * HEAD_DIM
C_KV_WIDTH = C_KV_HEADS * HEAD_DIM
C_WINDOW = 128
C_BLOCK = 128
FFN_HIDDEN = -(-8 * D_MODEL // (3 * 256)) * 256
N_AB_LAYERS = (DEPTH + 1) // 2
N_C_LAYERS = DEPTH // 2
ROPE_BASE = 10000.0
ROPE_AXIS_DIM = HEAD_DIM // 2
EPS = 1e-6
NEG_INF = -1e30

kernel_name = 'hybrid_diffusion_natten_pool_swa_step'


def rms_norm(x, g):
    x32 = x.astype(jnp.float32)
    y = x32 * lax.rsqrt(jnp.mean(x32 * x32, axis=-1, keepdims=True) + EPS)
    return (y * g.astype(jnp.float32)).astype(x.dtype)


def modulate(h, shift, scale):
    return h * (1 + scale) + shift


def axial_rope(x):
    n = x.shape[1]
    t = jnp.arange(n)
    pos = jnp.stack([t // GRID_W, t % GRID_W], axis=-1).astype(jnp.float32)
    half = ROPE_AXIS_DIM // 2
    inv = ROPE_BASE ** (-jnp.arange(half, dtype=jnp.float32) / half)
    ang = pos[:, :, None] * inv
    cos = jnp.cos(ang)[None, :, None].astype(x.dtype)
    sin = jnp.sin(ang)[None, :, None].astype(x.dtype)
    xa = x.reshape(*x.shape[:-1], 2, 2, half)
    x1, x2 = xa[..., 0, :], xa[..., 1, :]
    out = jnp.stack([x1 * cos - x2 * sin, x1 * sin + x2 * cos], axis=-2)
    return out.reshape(x.shape)


def dense_attention(q, k, v, sink):
    bsz, s, hq, hd = q.shape
    hkv = k.shape[2]
    g = hq // hkv
    qg = (q * hd ** -0.5).reshape(bsz, s, hkv, g, hd)
    logits = jnp.einsum('bqkgd,bpkd->bkgqp', qg, k).astype(jnp.float32)
    if sink is None:
        p = jax.nn.softmax(logits, axis=-1)
    else:
        sl = jnp.broadcast_to(sink.reshape(hkv, g)[None, :, :, None, None].astype(jnp.float32),
                              logits.shape[:-1] + (1,))
        p = jax.nn.softmax(jnp.concatenate([sl, logits], axis=-1), axis=-1)[..., 1:]
    out = jnp.einsum('bkgqp,bpkd->bqkgd', p.astype(v.dtype), v)
    return out.reshape(bsz, s, hq * hd)


def neighbourhood_attention(q, k, v, rpb, k_ctx, v_ctx):
    bsz, n, h, hd = q.shape
    rows = n // GRID_W
    kr = min(NA_ROWS, rows)
    n_cb = GRID_W // NA_QCOLS
    r = jnp.arange(rows)
    row_idx = jnp.clip(r - kr // 2, 0, rows - kr)[:, None] + jnp.arange(kr)[None, :]
    cb = jnp.arange(n_cb)
    col_idx = (jnp.clip(cb * NA_QCOLS - NA_COLS // 2, 0, GRID_W - NA_SPAN)[:, None]
               + jnp.arange(NA_SPAN)[None, :])
    qcol = cb[:, None] * NA_QCOLS + jnp.arange(NA_QCOLS)[None, :]
    qstart = jnp.clip(qcol - NA_COLS // 2, 0, GRID_W - NA_COLS)
    key_col = col_idx[:, None, :]
    valid = (key_col >= qstart[..., None]) & (key_col < qstart[..., None] + NA_COLS)
    d_row = row_idx - r[:, None] + (NA_ROWS - 1)
    d_col = jnp.clip(key_col - qcol[..., None] + (NA_COLS - 1), 0, 2 * NA_COLS - 2)
    bias = rpb[:, d_row[:, None, None, :, None], d_col[None, :, :, None, :]]
    bias = jnp.where(valid[None, None, :, :, None, :], bias.astype(jnp.float32), NEG_INF)
    bias = bias.reshape(h, rows, n_cb, NA_QCOLS, kr * NA_SPAN).transpose(1, 2, 0, 3, 4)

    def gather_band(t):
        g = t.reshape(bsz, rows, GRID_W, h, hd)[:, row_idx]
        g = g[:, :, :, col_idx]
        return g.transpose(0, 1, 3, 2, 4, 5, 6).reshape(bsz, rows, n_cb, kr * NA_SPAN, h, hd)

    kb = gather_band(k)
    vb = gather_band(v)
    qb = (q * hd ** -0.5).reshape(bsz, rows, n_cb, NA_QCOLS, h, hd)
    s_nb = jnp.einsum('brcqhd,brckhd->brchqk', qb, kb).astype(jnp.float32) + bias[None]
    s_ctx = jnp.einsum('brcqhd,bphd->brchqp', qb, k_ctx).astype(jnp.float32)
    p = jax.nn.softmax(jnp.concatenate([s_ctx, s_nb], axis=-1), axis=-1).astype(v.dtype)
    n_ctx = k_ctx.shape[1]
    out = (jnp.einsum('brchqp,bphd->brcqhd', p[..., :n_ctx], v_ctx)
           + jnp.einsum('brchqk,brckhd->brcqhd', p[..., n_ctx:], vb))
    return out.reshape(bsz, n, h * hd)


def window_attention(q, k, v, sink, k_ctx, v_ctx):
    bsz, n, hq, hd = q.shape
    hkv = k.shape[2]
    g = hq // hkv
    nb = n // C_BLOCK
    pad = ((0, 0), (C_BLOCK, C_BLOCK), (0, 0), (0, 0))

    def band(t):
        tp = jnp.pad(t, pad).reshape(bsz, nb + 2, C_BLOCK, hkv, hd)
        return jnp.concatenate([tp[:, :nb], tp[:, 1:nb + 1], tp[:, 2:]], axis=2)

    kband = band(k)
    vband = band(v)
    qb = (q * hd ** -0.5).reshape(bsz, nb, C_BLOCK, hkv, g, hd)
    qpos = jnp.arange(n).reshape(nb, C_BLOCK)
    kpos = jnp.arange(nb)[:, None] * C_BLOCK - C_BLOCK + jnp.arange(3 * C_BLOCK)[None, :]
    valid = ((jnp.abs(qpos[:, :, None] - kpos[:, None, :]) <= C_WINDOW)
             & (kpos[:, None, :] >= 0) & (kpos[:, None, :] < n))
    s_nb = jnp.einsum('bnqkgd,bnjkd->bnkgqj', qb, kband).astype(jnp.float32)
    s_nb = jnp.where(valid[None, :, None, None], s_nb, NEG_INF)
    s_ctx = jnp.einsum('bnqkgd,bpkd->bnkgqp', qb, k_ctx).astype(jnp.float32)
    sl = jnp.broadcast_to(sink.reshape(hkv, g)[None, None, :, :, None, None].astype(jnp.float32),
                          s_nb.shape[:-1] + (1,))
    p = jax.nn.softmax(jnp.concatenate([sl, s_ctx, s_nb], axis=-1), axis=-1).astype(v.dtype)
    n_ctx = k_ctx.shape[1]
    out = (jnp.einsum('bnkgqp,bpkd->bnqkgd', p[..., 1:1 + n_ctx], v_ctx)
           + jnp.einsum('bnkgqj,bnjkd->bnqkgd', p[..., 1 + n_ctx:], vband))
    return out.reshape(bsz, n, hq * hd)


def multiscale_pool(u, w_pool, pool_scale):
    bsz, n, _ = u.shape
    ug = u.reshape(bsz, n, B_GROUPS, B_GROUP_DIM).astype(jnp.float32)
    csum = jnp.concatenate([jnp.zeros_like(ug[:, :1]), jnp.cumsum(ug, axis=1)], axis=1)
    win = jnp.array(POOL_WINDOWS, dtype=jnp.int32)
    t = jnp.arange(n, dtype=jnp.int32)[:, None]
    lo = jnp.clip(t - win // 2, 0, n)
    hi = jnp.clip(t - win // 2 + win, 0, n)
    gi = jnp.arange(B_GROUPS)
    total = csum[:, hi, gi] - csum[:, lo, gi]
    count = (hi - lo).astype(jnp.float32)[None, :, :, None]
    pooled = (total / count - ug).astype(u.dtype)
    y = jnp.einsum('bngc,gcd->bngd', pooled, w_pool)
    return y.reshape(bsz, n, B_WIDTH) * pool_scale


def split_ab(h, w_in):
    bsz, n, _ = h.shape
    proj = h @ w_in
    q = proj[..., :A_WIDTH].reshape(bsz, n, A_HEADS, HEAD_DIM)
    k = proj[..., A_WIDTH:2 * A_WIDTH].reshape(bsz, n, A_HEADS, HEAD_DIM)
    v = proj[..., 2 * A_WIDTH:3 * A_WIDTH].reshape(bsz, n, A_HEADS, HEAD_DIM)
    u = proj[..., 3 * A_WIDTH:]
    return q, k, v, u


def split_c(h, w_in):
    bsz, n, _ = h.shape
    proj = h @ w_in
    q = proj[..., :C_Q_WIDTH].reshape(bsz, n, C_HEADS, HEAD_DIM)
    k = proj[..., C_Q_WIDTH:C_Q_WIDTH + C_KV_WIDTH].reshape(bsz, n, C_KV_HEADS, HEAD_DIM)
    v = proj[..., C_Q_WIDTH + C_KV_WIDTH:].reshape(bsz, n, C_KV_HEADS, HEAD_DIM)
    return q, k, v


def swiglu(h, w_gate_up, w_down):
    gu = h @ w_gate_up
    return (jax.nn.silu(gu[..., :FFN_HIDDEN]) * gu[..., FFN_HIDDEN:]) @ w_down


def setup_inputs(seed: int = 0) -> dict:
    key = jax.random.key(seed)
    ks = jax.random.split(key, 24)

    def nrm(k, shape, scale):
        return jax.random.normal(k, shape, jnp.float32) * scale

    return {
        'x_prompt': nrm(ks[0], (BATCH, SEQ, D_MODEL), 1.0),
        'x_sample': nrm(ks[1], (DEC_BATCH, DEC_SEQ, D_MODEL), 1.0),
        'cache_a_k': nrm(ks[2], (DEC_BATCH, N_AB_LAYERS, PAST_LEN, A_HEADS, HEAD_DIM), 1.0),
        'cache_a_v': nrm(ks[3], (DEC_BATCH, N_AB_LAYERS, PAST_LEN, A_HEADS, HEAD_DIM), 1.0),
        'cache_c_k': nrm(ks[4], (DEC_BATCH, N_C_LAYERS, PAST_LEN, C_KV_HEADS, HEAD_DIM), 1.0),
        'cache_c_v': nrm(ks[5], (DEC_BATCH, N_C_LAYERS, PAST_LEN, C_KV_HEADS, HEAD_DIM), 1.0),
        'c': nrm(ks[6], (DEC_BATCH, D_MODEL), 1.0),
        'c_ctx': nrm(ks[7], (D_MODEL,), 1.0),
        'w_mod': nrm(ks[8], (DEPTH, D_MODEL, 6 * D_MODEL), D_MODEL ** -0.5),
        'b_mod': nrm(ks[9], (DEPTH, 6 * D_MODEL), 0.02),
        'norm_mix': 1.0 + nrm(ks[10], (DEPTH, D_MODEL), 0.05),
        'norm_ffn': 1.0 + nrm(ks[11], (DEPTH, D_MODEL), 0.05),
        'w_in_ab': nrm(ks[12], (N_AB_LAYERS, D_MODEL, 3 * A_WIDTH + B_WIDTH), D_MODEL ** -0.5),
        'rpb_a': nrm(ks[13], (N_AB_LAYERS, A_HEADS, 2 * NA_ROWS - 1, 2 * NA_COLS - 1), 0.5),
        'w_pool': nrm(ks[14], (N_AB_LAYERS, B_GROUPS, B_GROUP_DIM, B_GROUP_DIM), B_GROUP_DIM ** -0.5),
        'pool_scale': 1.0 + nrm(ks[15], (N_AB_LAYERS, B_WIDTH), 0.05),
        'w_out_ab': nrm(ks[16], (N_AB_LAYERS, MIX_WIDTH, D_MODEL), MIX_WIDTH ** -0.5),
        'w_in_c': nrm(ks[17], (N_C_LAYERS, D_MODEL, C_Q_WIDTH + 2 * C_KV_WIDTH), D_MODEL ** -0.5),
        'sink_c': nrm(ks[18], (N_C_LAYERS, C_HEADS), 1.0),
        'w_out_c': nrm(ks[19], (N_C_LAYERS, C_Q_WIDTH, D_MODEL), C_Q_WIDTH ** -0.5),
        'w_gate_up': nrm(ks[20], (DEPTH, D_MODEL, 2 * FFN_HIDDEN), D_MODEL ** -0.5),
        'w_down': nrm(ks[21], (DEPTH, FFN_HIDDEN, D_MODEL), FFN_HIDDEN ** -0.5),
        'norm_final': 1.0 + nrm(ks[22], (D_MODEL,), 0.05),
    }


def reference(x_prompt, x_sample, cache_a_k, cache_a_v, cache_c_k, cache_c_v, c, c_ctx,
              w_mod, b_mod, norm_mix, norm_ffn, w_in_ab, rpb_a, w_pool, pool_scale, w_out_ab,
              w_in_c, sink_c, w_out_c, w_gate_up, w_down, norm_final):
    xp = x_prompt
    xs = x_sample
    a_k_list, a_v_list, c_k_list, c_v_list = [], [], [], []
    for l in range(DEPTH):
        mod_p = jax.nn.silu(c_ctx) @ w_mod[l] + b_mod[l]
        mod_s = (jax.nn.silu(c) @ w_mod[l] + b_mod[l])[:, None, :]
        sh1_p, sc1_p, g1_p, sh2_p, sc2_p, g2_p = jnp.split(mod_p, 6, axis=-1)
        sh1_s, sc1_s, g1_s, sh2_s, sc2_s, g2_s = jnp.split(mod_s, 6, axis=-1)

        hp = modulate(rms_norm(xp, norm_mix[l]), sh1_p, sc1_p)
        hs = modulate(rms_norm(xs, norm_mix[l]), sh1_s, sc1_s)
        if l % 2 == 0:
            i = l // 2
            qp, kp, vp, up = split_ab(hp, w_in_ab[i])
            op = jnp.concatenate([dense_attention(qp, kp, vp, None),
                                  multiscale_pool(up, w_pool[i], pool_scale[i])], axis=-1) @ w_out_ab[i]
            a_k_list.append(kp)
            a_v_list.append(vp)
            qs, ks_, vs, us = split_ab(hs, w_in_ab[i])
            os_ = jnp.concatenate([neighbourhood_attention(qs, ks_, vs, rpb_a[i], cache_a_k[:, i], cache_a_v[:, i]),
                                   multiscale_pool(us, w_pool[i], pool_scale[i])], axis=-1) @ w_out_ab[i]
        else:
            j = l // 2
            qp, kp, vp = split_c(hp, w_in_c[j])
            op = dense_attention(qp, kp, vp, sink_c[j]) @ w_out_c[j]
            c_k_list.append(kp)
            c_v_list.append(vp)
            qs, ks_, vs = split_c(hs, w_in_c[j])
            os_ = window_attention(axial_rope(qs), axial_rope(ks_), vs, sink_c[j],
                                   cache_c_k[:, j], cache_c_v[:, j]) @ w_out_c[j]
        xp = xp + g1_p * op
        xs = xs + g1_s * os_

        hp = modulate(rms_norm(xp, norm_ffn[l]), sh2_p, sc2_p)
        hs = modulate(rms_norm(xs, norm_ffn[l]), sh2_s, sc2_s)
        xp = xp + g2_p * swiglu(hp, w_gate_up[l], w_down[l])
        xs = xs + g2_s * swiglu(hs, w_gate_up[l], w_down[l])

    y_prompt = rms_norm(xp, norm_final)
    y_sample = rms_norm(xs, norm_final)
    new_a_k = jnp.stack(a_k_list, axis=1)
    new_a_v = jnp.stack(a_v_list, axis=1)
    new_c_k = jnp.stack(c_k_list, axis=1)
    new_c_v = jnp.stack(c_v_list, axis=1)
    return (y_prompt, y_sample, new_a_k, new_a_v, new_c_k, new_c_v)
```

```python
from contextlib import ExitStack
import numpy as np
import concourse.bass as bass
import concourse.mybir as mybir
from concourse.bass_utils import run_bass_kernel_spmd

F32 = mybir.dt.float32
BF16 = mybir.dt.bfloat16
AF = mybir.ActivationFunctionType
ALU = mybir.AluOpType

ENGS = ("pe", "act", "dve", "pool", "sp")
DMA_CH = 10
NEG = -30000.0
NSLOT = 6
EPS = 1e-6
NT = 768
NP = 512
FFN = 2816


class _Op:
    __slots__ = ("eng", "fn", "reads", "writes", "dma", "deps", "idx", "sig", "chan", "chan_cnt")


class Sched:
    def __init__(self):
        self.ops = []
        self.last_w = {}
        self.readers = {}
        self.ndma = {e: 0 for e in ENGS}
        self.ncc = 0
        self.cc_inc = 1

    def add(self, eng, fn, reads=(), writes=(), dma=False):
        o = _Op()
        o.eng, o.fn, o.dma = eng, fn, dma
        o.reads, o.writes = tuple(reads), tuple(writes)
        o.idx = len(self.ops)
        o.sig = None
        o.chan = None
        deps = set()
        for r in o.reads:
            w = self.last_w.get(r)
            if w is not None:
                deps.add(w)
        for w_ in o.writes:
            w = self.last_w.get(w_)
            if w is not None:
                deps.add(w)
            for rd in self.readers.get(w_, ()):
                deps.add(rd)
        for r in o.reads:
            self.readers.setdefault(r, []).append(o.idx)
        for w_ in o.writes:
            self.last_w[w_] = o.idx
            self.readers[w_] = []
        keep = set()
        for d in deps:
            p = self.ops[d]
            if not p.dma and not o.dma and p.eng == o.eng:
                if o.eng == "pe":
                    continue
            keep.add(d)
        o.deps = keep
        if dma == "cc":
            self.ncc += 1
            o.chan = "cc"
            o.chan_cnt = self.cc_inc * self.ncc
        elif dma:
            n = self.ndma[eng]
            self.ndma[eng] = n + 1
            o.chan = n % DMA_CH
            o.chan_cnt = 16 * (n // DMA_CH + 1)
        self.ops.append(o)
        return o

    def emit(self, nc):
        ops = self.ops
        needed = set()
        for o in ops:
            needed |= o.deps
        cnt = {e: 0 for e in ENGS}
        for o in ops:
            if o.dma:
                continue
            if o.idx in needed:
                cnt[o.eng] += 1
                o.sig = cnt[o.eng]
        with ExitStack() as es:
            esem = {e: es.enter_context(nc.semaphore("s_" + e)) for e in ENGS}
            dsem = {}
            for e in ENGS:
                for c in range(min(DMA_CH, self.ndma[e])):
                    dsem[(e, c)] = es.enter_context(nc.semaphore("d_%s%d" % (e, c)))
            if self.ncc:
                dsem[("pool", "cc")] = es.enter_context(nc.semaphore("d_cc"))
            waits = {}
            waited = {e: {} for e in ENGS}
            for o in ops:
                wl = {}
                if o.dma == "cc":
                    if o.chan_cnt > self.cc_inc:
                        wl[("d", o.eng, o.chan)] = o.chan_cnt - self.cc_inc
                elif o.dma and o.chan_cnt > 16:
                    wl[("d", o.eng, o.chan)] = o.chan_cnt - 16
                for d in o.deps:
                    p = ops[d]
                    if p.dma:
                        key, val = ("d", p.eng, p.chan), p.chan_cnt
                    else:
                        key, val = ("e", p.eng), p.sig
                    if wl.get(key, 0) < val:
                        wl[key] = val
                out = []
                for key, val in wl.items():
                    if waited[o.eng].get(key, 0) >= val:
                        continue
                    waited[o.eng][key] = val
                    out.append((key, val))
                waits[o.idx] = out

            def run(engname, eng):
                for o in ops:
                    if o.eng != engname:
                        continue
                    for key, val in waits[o.idx]:
                        s = esem[key[1]] if key[0] == "e" else dsem[(key[1], key[2])]
                        eng.wait_ge(s, val)
                    ins = o.fn(eng)
                    if o.dma == "cc":
                        ins.then_inc(dsem[(o.eng, o.chan)], self.cc_inc)
                    elif o.dma:
                        ins.then_inc(dsem[(o.eng, o.chan)], 16)
                    elif o.sig is not None:
                        ins.then_inc(esem[o.eng], 1)
                if engname == "pool" and self.ncc:
                    eng.wait_ge(dsem[("pool", "cc")], self.cc_inc * self.ncc)
                n = self.ndma[engname]
                for c in range(min(DMA_CH, n)):
                    k = (n - 1 - c) // DMA_CH + 1
                    eng.wait_ge(dsem[(engname, c)], 16 * k)

            with nc.Block() as block:
                @block.tensor
                def _(eng):
                    run("pe", eng)

                @block.scalar
                def _(eng):
                    run("act", eng)

                @block.vector
                def _(eng):
                    run("dve", eng)

                @block.gpsimd
                def _(eng):
                    run("pool", eng)

                @block.sync
                def _(eng):
                    run("sp", eng)


def build_program(depth=4, stop=None):
    nc = bass.Bass("TRN2", target_bir_lowering=False)

    def din(name, shape):
        return nc.dram_tensor(name, list(shape), F32, kind="ExternalInput").ap()

    def dout(name, shape):
        return nc.dram_tensor(name, list(shape), F32, kind="ExternalOutput").ap()

    xin = din("xin", [NT, 1024])
    cvT = din("cvT", [128, 16])
    w_mod = din("w_mod", [4, 1024, 1536])
    bmodT = din("bmodT", [128, 48])
    gmixT = din("gmixT", [128, 32])
    gffnT = din("gffnT", [128, 32])
    gfinT = din("gfinT", [128, 8])
    w_in_ab = din("w_in_ab", [2, 1024, 2048])
    w_out_ab = din("w_out_ab", [2, 1024, 1024])
    w_pool = din("w_pool", [2, 4, 128, 128])
    pscT = din("pscT", [128, 8])
    w_in_cx = din("w_in_cx", [2, 1024, 3584])
    w_out_c = din("w_out_c", [2, 1024, 1024])
    sinkT = din("sinkT", [128, 16])
    w_gu = din("w_gu", [4, 1024, 2 * FFN])
    w_dn = din("w_dn", [4, FFN, 1024])
    cak = din("cak", [2, 512, 512])
    cav = din("cav", [2, 512, 512])
    cck = din("cck", [2, 512, 512])
    ccv = din("ccv", [2, 512, 512])
    nab = din("nab", [2, 8, 128, 2048])
    c_ident = din("c_ident", [128, 128])
    c_cmask = din("c_cmask", [128, 2048])
    c_ropeC = din("c_ropeC", [128, 256])
    c_ropeS = din("c_ropeS", [128, 256])
    c_wmask = din("c_wmask", [128, 2048])
    c_poolA = din("c_poolA", [128, 4 * 5 * 128])
    c_poolAs = din("c_poolAs", [4, 128, 2048])

    o_yp = dout("o_yp", [512, 1024])
    o_ys = dout("o_ys", [256, 1024])
    o_nak = dout("o_nak", [2, 2, 256, 512])
    o_nav = dout("o_nav", [2, 2, 256, 512])
    o_nck = dout("o_nck", [2, 2, 256, 256])
    o_ncv = dout("o_ncv", [2, 2, 256, 256])

    SLW = [3072, 2048, 3072, 2048]
    slab = [nc.dram_tensor("slab%d" % l, [128, SLW[l]], BF16) for l in range(4)]
    gath = [nc.dram_tensor("gath%d" % l, [512, SLW[l]], BF16) for l in range(4)]
    RG = [[0, 1, 2, 3], [4, 5, 6, 7]]
    mslab0 = nc.dram_tensor("mslab0", [128, 24], F32)
    mgath0 = nc.dram_tensor("mgath0", [512, 24], F32)
    mslab1 = nc.dram_tensor("mslab1", [128, 72], F32)
    mgath1 = nc.dram_tensor("mgath1", [512, 72], F32)

    CH = [(0, 512), (512, 256)]

    S = Sched()
    with ExitStack() as es:
        def sb(name, shape, dt):
            return es.enter_context(nc.sbuf_tensor(name, list(shape), dt))

        xT = sb("xT", [128, 8, NT], F32)
        HM = sb("HM", [128, 8, NT], BF16)
        U1 = sb("U1", [128, 22 * NT], BF16)
        qkT = U1[:, 0:22 * NT].rearrange("p (a t) -> p a t", a=22)
        TM = sb("TM", [128, 4, 512], BF16)
        ring = [sb("ring%d" % i, [128, 2048], BF16) for i in range(NSLOT)]
        KsT = sb("KsT", [128, 4, 1024], BF16)
        Vs = sb("Vs", [128, 8, 512], BF16)
        Us = sb("Us", [128, 8, 512], BF16)
        SL = sb("SL", [128, 3072], BF16)
        Eb = [sb("E%d" % i, [128, 1024], BF16) for i in range(2)]
        T2 = sb("T2", [128, 2, 2048], BF16)
        T2s = sb("T2s", [128, 2048], BF16)
        qz = [sb("qz%d" % i_, [128, 2, 256], BF16) for i_ in range(2)]
        cmask = sb("cmask", [128, 2048], BF16)
        wmask = sb("wmask", [128, 8, 256], BF16)
        poolAs = sb("poolAs", [128, 8, 256], BF16)
        ctxKT = sb("ctxKT", [128, 4, 512], BF16)
        ctxV = sb("ctxV", [128, 4, 512], BF16)
        ropeC = sb("ropeC", [128, 256], F32)
        ropeS = sb("ropeS", [128, 256], F32)
        stg = [sb("stg%d" % i, [128, 1024], F32) for i in range(3)]
        sq = [sb("sq%d" % i, [128, 512], BF16) for i in range(2)]
        tmp = [sb("tmp%d" % i, [128, 512], F32) for i in range(3)]
        rt = sb("rt", [128, 512], F32)
        rstd = sb("rstd", [128, 512], F32)
        rden = sb("rden", [128, 512], F32)
        pooled = [sb("pooled%d" % i, [128, 512], BF16) for i in range(2)]
        ident = sb("ident", [128, 128], F32)
        identb = sb("identb", [128, 128], BF16)
        ones = sb("ones", [128, 128], BF16)
        poolA = sb("poolA", [128, 4, 5, 128], BF16)
        wpool = sb("wpool", [128, 4, 128], BF16)
        modT = sb("modT", [128, 4, 48, 2], F32)
        modP = sb("modP", [128, 48, 2], F32)
        Aab = sb("Aab", [128, 4, 2, 8, 2], F32)
        bmod = sb("bmod", [128, 48], F32)
        gmix = sb("gmix", [128, 4, 8], F32)
        gffn = sb("gffn", [128, 4, 8], F32)
        gfin = sb("gfin", [128, 8], F32)
        psc = sb("psc", [128, 2, 4], F32)
        sinkE = sb("sinkE", [128, 2, 8], F32)
        cv = sb("cv", [128, 8, 2], F32)
        cvb = sb("cvb", [128, 8, 2], BF16)
        ps = [es.enter_context(nc.psum_tensor("ps%d" % i, [128, 512], F32)) for i in range(8)]

        st = {"ring": 0, "bank": 0, "alt": 0, "tmp": 0, "sq": 0, "stg": 0, "reserved": set()}

        def PS(b):
            return ("ps", b)

        def nextbank():
            while True:
                b = st["bank"]
                st["bank"] = (b + 1) % 8
                if b not in st["reserved"]:
                    return b

        def evac_eng():
            st["alt"] ^= 1
            return "act" if st["alt"] else "dve"

        def ACT(out, in_, func, reads, writes, bias=None, scale=None):
            kw = {}
            if bias is not None:
                kw["bias"] = bias
            if scale is not None:
                kw["scale"] = scale
            S.add("act", lambda e: e.activation(out, in_, func, **kw), reads=reads, writes=writes)

        def TT(out, a, b_, op, reads, writes):
            S.add("dve", lambda e: e.tensor_tensor(out, a, b_, op), reads=reads, writes=writes)

        def TS(out, a, s1, op0, reads, writes):
            S.add("dve", lambda e: e.tensor_scalar(out, a, s1, None, op0), reads=reads, writes=writes)

        def STT(out, in0, scalar, in1, op0, op1, reads, writes):
            S.add("dve", lambda e: e.scalar_tensor_tensor(out, in0, scalar, in1, op0, op1), reads=reads, writes=writes)

        def RECIP(out, in_, reads, writes):
            S.add("dve", lambda e: e.reciprocal(out, in_), reads=reads, writes=writes)

        def TR(out, in_, reads, writes):
            S.add("pe", lambda e: e.transpose(out, in_, ident[:]), reads=list(reads) + ["ident"], writes=writes)

        def DMA(q, out, in_, reads=(), writes=()):
            S.add(q, lambda e: e.dma_start(out=out, in_=in_), reads=reads, writes=writes, dma=True)

        def copy_op(eng, out, in_, reads, writes):
            if eng == "act":
                ACT(out, in_, AF.Identity, reads, writes)
            else:
                S.add("dve", lambda e: e.tensor_copy(out, in_), reads=reads, writes=writes)

        def mm(out, lhsT, rhs, start, stop, reads, pskey):
            rr = []
            for r_ in reads:
                if hasattr(r_, "keys") and not isinstance(r_, (tuple, str)):
                    rr.extend(r_.keys)
                else:
                    rr.append(r_)
            S.add("pe", lambda e: e.matmul(out, lhsT, rhs, start=start, stop=stop), reads=rr, writes=[pskey])

        class Piece:
            def __init__(self, views, keys):
                self.views, self.keys = views, keys

            def __getitem__(self, idx):
                p, k, cs = idx
                a_, b_ = cs.start, cs.stop
                h = a_ // 256
                assert (b_ - 1) // 256 == h
                return self.views[h][:, k, a_ - h * 256:b_ - h * 256]

        class PieceKeys:
            def __init__(self, keys):
                self.keys = keys

        def piece(src2d, KT, NC):
            assert KT <= 8
            views, keys = [], []
            for h in range((NC + 255) // 256):
                ncol = min(256, NC - h * 256)
                s_ = st["ring"]
                st["ring"] = (s_ + 1) % NSLOT
                view = ring[s_][:, 0:KT * ncol].rearrange("p (k n) -> p k n", k=KT)
                DMA("pool", view, src2d[:, h * 256:h * 256 + ncol].rearrange("(k p) n -> p k n", p=128), writes=[("w", s_)])
                views.append(view)
                keys.append(("w", s_))
            return Piece(views, keys), PieceKeys(keys)

        def load_f32(dst, src, key):
            DMA("sp", dst, src, writes=[key])

        def load_cast(dst, src, key):
            DMA("pool", dst, src, writes=[key])

        def next_tmp():
            ti = st["tmp"]
            st["tmp"] = (ti + 1) % 3
            return ti

        def next_stg():
            si = st["stg"]
            st["stg"] = (si + 1) % 3
            return si

        def csl(c):
            return slice(CH[c][0], CH[c][0] + CH[c][1])

        def cw(c):
            return CH[c][1]

        load_f32(cv[:].rearrange("p k j -> p (k j)"), cvT, "cv")
        load_f32(bmod[:], bmodT, "bmod")
        load_f32(ident[:], c_ident, "ident")
        load_f32(gmix[:].rearrange("p l k -> p (l k)"), gmixT, "gmix")
        load_f32(gffn[:].rearrange("p l k -> p (l k)"), gffnT, "gffn")
        load_f32(gfin[:], gfinT, "gfin")
        load_f32(psc[:].rearrange("p l g -> p (l g)"), pscT, "psc")
        load_f32(sinkE[:].rearrange("p l g -> p (l g)"), sinkT, "sinkE")
        load_f32(ropeC[:], c_ropeC, "ropeC")
        load_f32(ropeS[:], c_ropeS, "ropeS")
        S.add("dve", lambda e: e.memset(ones[:], 1.0), writes=["ones"])
        for i_ in range(2):
            S.add("dve", lambda e, i_=i_: e.memset(qz[i_][:], 0.0), writes=[("qz", i_, 0), ("qz", i_, 1)])
        ACT(cvb[:].rearrange("p k j -> p (k j)"), cv[:].rearrange("p k j -> p (k j)"), AF.Silu, ["cv"], ["cvb"])
        ACT(sinkE[:].rearrange("p l g -> p (l g)"), sinkE[:].rearrange("p l g -> p (l g)"), AF.Exp, ["sinkE"], ["sinkE"])
        TS(gmix[:].rearrange("p l k -> p (l k)"), gmix[:].rearrange("p l k -> p (l k)"), 32.0, ALU.mult, ["gmix"], ["gmix"])
        TS(gffn[:].rearrange("p l k -> p (l k)"), gffn[:].rearrange("p l k -> p (l k)"), 32.0, ALU.mult, ["gffn"], ["gffn"])
        TS(gfin[:], gfin[:], 32.0, ALU.mult, ["gfin"], ["gfin"])

        def late_consts():
            load_cast(identb[:], c_ident, "identb")
            load_cast(cmask[:], c_cmask, "cmask")
            load_cast(wmask[:].rearrange("p a k -> p (a k)"), c_wmask, "wmask")
            load_cast(poolA[:].rearrange("p g a t -> p (g a t)"), c_poolA, "poolA")

        def aab_layer(l):
            for which, g_, off, nm in ((0, gmix, 8, "gmix"), (1, gffn, 32, "gffn")):
                for j in range(2):
                    STT(Aab[:, l, which, :, j], modT[:, l, off:off + 8, j], 1.0, g_[:, l, :], ALU.add, ALU.mult,
                        [("modT", l), nm], [("Aab", l)])

        def mod_prologue():
            b = nextbank()
            for pc in range(3):
                wv, wk = piece(w_mod[0, :, pc * 512:(pc + 1) * 512], 8, 512)
                for mm_ in range(4):
                    m = pc * 4 + mm_
                    for k in range(8):
                        mm(ps[b][:, m * 2:(m + 1) * 2], wv[:, k, mm_ * 128:(mm_ + 1) * 128], cvb[:, k, :],
                           k == 0, k == 7, [wk, "cvb"], PS(b))
            for j in range(2):
                TT(modP[:, 0:12, j], ps[b][:, 0:24].rearrange("p (m j) -> p m j", j=2)[:, :, j], bmod[:, 0:12], ALU.add,
                   [PS(b), "bmod"], ["modP0"])
            DMA("sp", mslab0.ap(), modP[:, 0:12, :].rearrange("p m j -> p (m j)"), reads=["modP0"], writes=["mslab0"])
            S.add("pool", lambda e: e.collective_compute("AllGather", ALU.bypass, replica_groups=RG,
                                                         ins=[mslab0.ap().opt()], outs=[mgath0.ap().opt()]),
                  reads=["mslab0"], writes=["mgath0"], dma="cc")

        def mod_prologue_recv():
            for r in range(4):
                DMA("sp", modT[:, 0, r * 12:(r + 1) * 12, :],
                    mgath0.ap()[r * 128:(r + 1) * 128, :].rearrange("p (m j) -> p m j", m=12),
                    reads=["mgath0"], writes=[("modT", 0)])
            aab_layer(0)

        mod_todo = [(l, h) for l in range(1, 4) for h in range(6)]

        def mod_step():
            if not mod_todo or depth < 2:
                return
            l, h = mod_todo.pop(0)
            wv, wk = piece(w_mod[l, :, h * 256:(h + 1) * 256], 8, 256)
            b = nextbank()
            for mm_ in range(2):
                for k in range(8):
                    mm(ps[b][:, mm_ * 2:(mm_ + 1) * 2], wv[:, k, mm_ * 128:(mm_ + 1) * 128], cvb[:, k, :],
                       k == 0, k == 7, [wk, "cvb"], PS(b))
            i0 = l * 12 + h * 2
            for j in range(2):
                TT(modP[:, i0:i0 + 2, j], ps[b][:, 0:4].rearrange("p (m j) -> p m j", j=2)[:, :, j], bmod[:, i0:i0 + 2], ALU.add,
                   [PS(b), "bmod"], ["modP1"])

        mod_state = {"finished": False}

        def mod_finish():
            if depth < 2 or mod_state["finished"]:
                return
            mod_state["finished"] = True
            while mod_todo:
                mod_step()
            DMA("sp", mslab1.ap(), modP[:, 12:48, :].rearrange("p m j -> p (m j)"), reads=["modP1"], writes=["mslab1"])
            S.add("pool", lambda e: e.collective_compute("AllGather", ALU.bypass, replica_groups=RG,
                                                         ins=[mslab1.ap().opt()], outs=[mgath1.ap().opt()]),
                  reads=["mslab1"], writes=["mgath1"], dma="cc")

        def mod_finish_recv():
            if depth < 2:
                return
            for r in range(4):
                DMA("sp", modT[:, 1:4, r * 12:(r + 1) * 12, :],
                    mgath1.ap()[r * 128:(r + 1) * 128, :].rearrange("p (l m j) -> p l m j", l=3, m=12),
                    reads=["mgath1"], writes=[("modT", 1), ("modT", 2), ("modT", 3)])
            for l in range(1, 4):
                aab_layer(l)

        SSB = {0: 6, 1: 7}

        def ss_accum(k, c):
            w = cw(c)
            si = st["sq"]
            st["sq"] ^= 1
            ACT(sq[si][:, 0:w], xT[:, k, csl(c)], AF.Square, [("x", k, c)], [("sq", si)])
            mm(ps[SSB[c]][:, 0:w], ones[:], sq[si][:, 0:w], k == 0, k == 7, ["ones", ("sq", si)], PS(SSB[c]))

        ss_pending = []

        def ss_defer(k, c):
            ss_pending.append((k, c))
            if len(ss_pending) > 4:
                ss_accum(*ss_pending.pop(0))

        def ss_flush():
            while ss_pending:
                ss_accum(*ss_pending.pop(0))

        def rstd_finish(c):
            w = cw(c)
            ACT(rt[:, 0:w], ps[SSB[c]][:, 0:w], AF.Ln, [PS(SSB[c])], ["rt"], bias=1024 * EPS, scale=1.0)
            ACT(rstd[:, 0:w], rt[:, 0:w], AF.Exp, ["rt"], ["rstd"], scale=-0.5)

        def rstd_chunk(c):
            for k in range(8):
                ss_accum(k, c)
            rstd_finish(c)

        def norm_phase(l, which, off_b, pre=False):
            for c in range(2):
                j = c
                w = cw(c)
                if pre:
                    rstd_finish(c)
                else:
                    rstd_chunk(c)
                for k in range(8):
                    ti = next_tmp()
                    TT(tmp[ti][:, 0:w], xT[:, k, csl(c)], rstd[:, 0:w], ALU.mult, [("x", k, c), "rstd"], [("tmp", ti)])
                    ACT(HM[:, k, csl(c)], tmp[ti][:, 0:w], AF.Identity, [("tmp", ti), ("modT", l), ("Aab", l)], [("hm", k, c)],
                        bias=modT[:, l, off_b + k, j:j + 1], scale=Aab[:, l, which, k, j:j + 1])

        def std_group(wv, wk, m, c):
            b = nextbank()
            w = cw(c)
            for k in range(8):
                mm(ps[b][:, 0:w], wv[:, k, m * 128:(m + 1) * 128], HM[:, k, csl(c)],
                   k == 0, k == 7, [wk, ("hm", k, c)], PS(b))
            return b

        def tok_proj(wv, wk, ncols, blocks, on_block):
            for blk in blocks:
                c = 0 if blk < 4 else 1
                b = nextbank()
                for h in range(ncols // 256):
                    for k in range(8):
                        mm(ps[b][:, h * 256:(h + 1) * 256], HM[:, k, blk * 128:(blk + 1) * 128], wv[:, k, h * 256:(h + 1) * 256],
                           k == 0, k == 7, [wk, ("hm", k, c)], PS(b))
                on_block(blk, b)

        def out_stage(b, ncols, col0, dst_dram, eng=None):
            si = next_stg()
            copy_op(eng or evac_eng(), stg[si][:, 0:ncols], ps[b][:, col0:col0 + ncols], [PS(b)], [("stg", si)])
            DMA("sp", dst_dram, stg[si][:, 0:ncols], reads=[("stg", si)])

        att = {"S": 0, "OD": 0, "qz": 0}

        def attend_A(qz_ap, q_reads, N, kblocks):
            sbuf_i = att["S"]
            att["S"] ^= 1
            banks = (0, 1) if sbuf_i == 0 else (2, 3)
            nkb = len(kblocks)
            assert nkb * N <= 1024
            per_bank = 512 // N
            for i, kb in enumerate(kblocks):
                bk = banks[i // per_bank]
                col = (i % per_bank) * N
                has_bias = kb[4] is not None
                mm(ps[bk][:, col:col + N], kb[0], qz_ap, True, not has_bias, list(kb[1]) + list(q_reads), PS(bk))
                if has_bias:
                    mm(ps[bk][:, col:col + N], identb[:], kb[4], False, True, ["identb"] + list(kb[5]), PS(bk))
            E = Eb[sbuf_i]
            nb_used = (nkb + per_bank - 1) // per_bank
            for bi in range(nb_used):
                ncol = min(nkb - bi * per_bank, per_bank) * N
                bk = banks[bi]
                ACT(E[:, bi * 512:bi * 512 + ncol], ps[bk][:, 0:ncol], AF.Exp, [PS(bk)], [("E", sbuf_i, bi)], scale=0.125)
            return sbuf_i

        def attend_B(sbuf_i, N, kblocks, obank, dbank, ocol, acc_start, acc_stop):
            E = Eb[sbuf_i]
            nkb = len(kblocks)
            per_bank = 512 // N
            for i, kb in enumerate(kblocks):
                bi = i // per_bank
                col = bi * 512 + (i % per_bank) * N
                mm(ps[obank][:, ocol:ocol + N], kb[2], E[:, col:col + N], acc_start and i == 0, acc_stop and i == nkb - 1,
                   list(kb[3]) + [("E", sbuf_i, bi)], PS(obank))
            for i, kb in enumerate(kblocks):
                bi = i // per_bank
                col = bi * 512 + (i % per_bank) * N
                mm(ps[dbank][:, ocol:ocol + N], ones[:], E[:, col:col + N], acc_start and i == 0, acc_stop and i == nkb - 1,
                   ["ones", ("E", sbuf_i, bi)], PS(dbank))

        def make_qz(q_tile, c0, q_key):
            bi = att["qz"]
            att["qz"] ^= 1
            for e_ in range(2):
                rows = slice(e_ * 64, (e_ + 1) * 64)
                S.add("dve", lambda e, e_=e_, rows=rows: e.tensor_copy(qz[bi][rows, e_, :], qkT[rows, q_tile, c0:c0 + 256]),
                      reads=[q_key], writes=[("qz", bi, e_)])
            return bi

        def od_banks():
            i = att["OD"]
            att["OD"] ^= 1
            return (4, 5) if i == 0 else (6, 7)

        def normalize(obank, dbank, dst_tile, c0, dst_key, sink_ap=None):
            if sink_ap is not None:
                ACT(rden[:], ps[dbank][:], AF.Ln, [PS(dbank), "sinkE"], ["rden"], bias=sink_ap, scale=1.0)
            else:
                ACT(rden[:], ps[dbank][:], AF.Ln, [PS(dbank)], ["rden"])
            ACT(rden[:], rden[:], AF.Exp, ["rden"], ["rden"], scale=-1.0)
            for e_ in range(2):
                rows = slice(e_ * 64, (e_ + 1) * 64)
                cols = slice(e_ * 256, (e_ + 1) * 256)
                TT(HM[rows, dst_tile, c0:c0 + 256], ps[obank][rows, cols], rden[rows, cols], ALU.mult,
                   [PS(obank), "rden"], [dst_key])

        def load_ctx(kd, vd, vcols):
            load_cast(ctxV[:, :, 0:vcols], vd.rearrange("(i p) f -> p i f", p=128), "ctxV")
            for blk in range(4):
                si = next_stg()
                load_f32(stg[si][:, 0:512], kd[blk * 128:(blk + 1) * 128, :], ("stg", si))
                b = nextbank()
                for j in range(4):
                    TR(ps[b][:, j * 128:(j + 1) * 128], stg[si][:, j * 128:(j + 1) * 128], [("stg", si)], [PS(b)])
                copy_op(evac_eng(), ctxKT[:, :, blk * 128:(blk + 1) * 128], ps[b][:].rearrange("p (j t) -> p j t", j=4),
                        [PS(b)], ["ctxKT"])

        def exchange(l, even):
            W = SLW[l]
            DMA("sp", slab[l].ap(), SL[:, 0:W], reads=["SLk", "SLv", "SLu"], writes=[("slab", l)])
            S.add("pool", lambda e: e.collective_compute("AllGather", ALU.bypass, replica_groups=RG,
                                                         ins=[slab[l].ap().opt()], outs=[gath[l].ap().opt()]),
                  reads=[("slab", l)], writes=[("gath", l)], dma="cc")

        def exchange_recv(l, even):
            g = gath[l].ap()
            vw = 512
            for r in range(4):
                rows = slice(r * 128, (r + 1) * 128)
                DMA("sp", KsT[:, :, r * 256:(r + 1) * 256], g[rows, 0:1024].rearrange("p (j t) -> p j t", j=4),
                    reads=[("gath", l)], writes=[("KsT", r)])
                DMA("sp", Vs[:, 2 * r:2 * r + 2, 0:vw], g[rows, 1024:1024 + 2 * vw].rearrange("p (b f) -> p b f", b=2),
                    reads=[("gath", l)], writes=[("Vs", r)])
                if even:
                    DMA("sp", Us[:, 2 * r:2 * r + 2, :], g[rows, 2048:3072].rearrange("p (b f) -> p b f", b=2),
                        reads=[("gath", l)], writes=[("Us", r)])

        def run_attention_units(units):
            nu = len(units)
            obs = [od_banks() for _ in units]
            qzb = [None] * nu
            calls = []
            for ui, u in enumerate(units):
                for e_ in range(2):
                    kbl = u["kbl_fn"](e_)
                    np_ = u["parts"]
                    per = len(kbl) // np_
                    for part in range(np_):
                        calls.append(dict(ui=ui, e=e_, kbl=kbl[part * per:(part + 1) * per], start=(part == 0), stop=(part == np_ - 1),
                                          last_of_head=(part == np_ - 1), last_of_unit=(e_ == 1 and part == np_ - 1)))

            def ensure_unit_ready(ui):
                if ui < nu and qzb[ui] is None:
                    u = units[ui]
                    qzb[ui] = make_qz(u["q_tile"], u["c0"], u["q_key"])

            def prep_head(ui, e_):
                if ui < nu and units[ui].get("prep") is not None:
                    units[ui]["prep"](e_)
            ensure_unit_ready(0)
            prep_head(0, 0)
            prep_head(0, 1)
            pending = None
            for c_ in calls:
                ui = c_["ui"]
                u = units[ui]
                ensure_unit_ready(ui)
                si = attend_A(qz[qzb[ui]][:, c_["e"], :], [("qz", qzb[ui], c_["e"])], 256, c_["kbl"])
                if c_["last_of_head"]:
                    prep_head(ui + 1, c_["e"])
                if c_["e"] == 0 and c_["start"]:
                    ensure_unit_ready(ui + 1)
                if pending is not None:
                    pc_, psi = pending
                    pob, pdb = obs[pc_["ui"]]
                    attend_B(psi, 256, pc_["kbl"], pob, pdb, pc_["e"] * 256, pc_["start"], pc_["stop"])
                    if pc_["last_of_unit"]:
                        pu = units[pc_["ui"]]
                        normalize(pob, pdb, pu["dst_tile"], pu["c0"], pu["dst_key"], sink_ap=pu.get("sink"))
                pending = (c_, si)
            pc_, psi = pending
            pob, pdb = obs[pc_["ui"]]
            attend_B(psi, 256, pc_["kbl"], pob, pdb, pc_["e"] * 256, pc_["start"], pc_["stop"])
            pu = units[pc_["ui"]]
            normalize(pob, pdb, pu["dst_tile"], pu["c0"], pu["dst_key"], sink_ap=pu.get("sink"))

        def sample_units(nheadpairs, kt_tile_fn, v_col_fn, ctx_tile_fn, ctxv_col_fn, table_fn, table_keys_fn,
                         sink_fn=None, prep_fn=None):
            units = []
            for jp in range(nheadpairs):
                def kbl_fn(e_, jp=jp):
                    kbl = []
                    for ib in range(4):
                        kbl.append((ctxKT[:, ctx_tile_fn(jp), ib * 128:(ib + 1) * 128], ["ctxKT"],
                                    ctxV[:, ib, ctxv_col_fn(jp)], ["ctxV"], None, None))
                    for m in range(8):
                        kbl.append((KsT[:, kt_tile_fn(jp), m * 128:(m + 1) * 128], [("KsT", m // 2)],
                                    Vs[:, m, v_col_fn(jp)], [("Vs", m // 2)],
                                    table_fn(jp, e_, m), table_keys_fn(jp, e_)))
                    return kbl
                units.append(dict(q_tile=jp, c0=CH[1][0], q_key=("qk", jp, 1), kbl_fn=kbl_fn, parts=3,
                                  dst_tile=jp, dst_key=("hm", jp, 1),
                                  sink=None if sink_fn is None else sink_fn(jp),
                                  prep=None if prep_fn is None else (lambda e_, jp=jp: prep_fn(jp, e_))))
            return units

        def prompt_units(nheadpairs, kt_tile_fn, v_col_fn, sink_fn=None):
            units = []
            for jp in range(nheadpairs):
                for s_ in range(2):
                    def kbl_fn(e_, jp=jp, s_=s_):
                        kbl = []
                        for kb in range(2):
                            blk = s_ * 2 + kb
                            kbl.append((qkT[:, kt_tile_fn(jp), blk * 128:(blk + 1) * 128], [("qk", kt_tile_fn(jp), 0)],
                                        TM[:, blk, v_col_fn(jp)], [("tm", blk)], None, None))
                        return kbl
                    units.append(dict(q_tile=jp, c0=s_ * 256, q_key=("qk", jp, 0), kbl_fn=kbl_fn, parts=1,
                                      dst_tile=jp, dst_key=("hm", jp, 0),
                                      sink=None if sink_fn is None else sink_fn(jp)))
            return units

        def resid_update(bank, m, c, gate_ap, l):
            w = cw(c)
            STT(xT[:, m, csl(c)], ps[bank][:, 0:w], gate_ap, xT[:, m, csl(c)], ALU.mult, ALU.add,
                [PS(bank), ("x", m, c), ("modT", l)], [("x", m, c)])

        def out_proj(l, wsrc, gate_off, src_fn):
            st["reserved"] = {6, 7}
            for pc in range(2):
                wv, wk = piece(wsrc[:, pc * 512:(pc + 1) * 512], 8, 512)
                for c in range(2):
                    w = cw(c)
                    for mm_ in range(4):
                        m = pc * 4 + mm_
                        b = nextbank()
                        for k in range(8):
                            ap_, key_ = src_fn(k, c)
                            mm(ps[b][:, 0:w], wv[:, k, mm_ * 128:(mm_ + 1) * 128], ap_, k == 0, k == 7, [wk, key_], PS(b))
                        resid_update(b, m, c, modT[:, l, gate_off + m, c:c + 1], l)
                        ss_defer(m, c)
            ss_flush()

        def ffn_phase(l):
            hid = qkT
            for t0 in range(0, 22, 2):
                gcol = t0 * 128
                wg, wgk = piece(w_gu[l, :, gcol:gcol + 256], 8, 256)
                wu, wuk = piece(w_gu[l, :, FFN + gcol:FFN + gcol + 256], 8, 256)
                for c in range(2):
                    w = cw(c)
                    for t in range(2):
                        bg = std_group(wg, wgk, t, c)
                        bu = std_group(wu, wuk, t, c)
                        ti = next_tmp()
                        ACT(tmp[ti][:, 0:w], ps[bg][:, 0:w], AF.Silu, [PS(bg)], [("tmp", ti)])
                        TT(hid[:, t0 + t, csl(c)], ps[bu][:, 0:w], tmp[ti][:, 0:w], ALU.mult,
                           [PS(bu), ("tmp", ti)], [("qk", t0 + t, c)])
                if l == 0:
                    mod_step()
                    if t0 < 14:
                        mod_step()
            if l == 0:
                mod_finish()
            st["reserved"] = {6, 7}
            kb_ = [0, 8, 16, 22]
            for pc in range(4):
                pcs = [piece(w_dn[l, kb_[j_] * 128:kb_[j_ + 1] * 128, pc * 256:(pc + 1) * 256], kb_[j_ + 1] - kb_[j_], 256)
                       for j_ in range(3)]
                for c in range(2):
                    w = cw(c)
                    for mm_ in range(2):
                        m = pc * 2 + mm_
                        b = nextbank()
                        for k in range(22):
                            j_ = max(jj for jj in range(3) if kb_[jj] <= k)
                            wv_, wk_ = pcs[j_]
                            mm(ps[b][:, 0:w], wv_[:, k - kb_[j_], mm_ * 128:(mm_ + 1) * 128], hid[:, k, csl(c)],
                               k == 0, k == 21, [wk_, ("qk", k, c)], PS(b))
                        resid_update(b, m, c, modT[:, l, 40 + m, c:c + 1], l)
                        ss_defer(m, c)
            ss_flush()
            if l == 0:
                mod_finish()
                mod_finish_recv()

        def mixer_ab(l, i):
            wv_k, wk_k = piece(w_in_ab[i, :, 512:1024], 8, 512)
            for m in range(4):
                b = std_group(wv_k, wk_k, m, 1)
                copy_op(evac_eng(), SL[:, m * 256:(m + 1) * 256], ps[b][:, 0:256], [PS(b)], ["SLk"])
            wv_v, wk_v = piece(w_in_ab[i, :, 1024:1536], 8, 512)
            tok_proj(wv_v, wk_v, 512, (4, 5),
                     lambda blk, b: copy_op(evac_eng(), SL[:, 1024 + (blk - 4) * 512:1024 + (blk - 3) * 512], ps[b][:], [PS(b)], ["SLv"]))
            wv_u, wk_u = piece(w_in_ab[i, :, 1536:2048], 8, 512)
            tok_proj(wv_u, wk_u, 512, (4, 5),
                     lambda blk, b: copy_op(evac_eng(), SL[:, 2048 + (blk - 4) * 512:2048 + (blk - 3) * 512], ps[b][:], [PS(b)], ["SLu"]))
            exchange(l, True)
            for m in range(4):
                b = std_group(wv_k, wk_k, m, 0)
                copy_op(evac_eng(), qkT[:, 4 + m, csl(0)], ps[b][:], [PS(b)], [("qk", 4 + m, 0)])
            tok_proj(wv_k, wk_k, 512, range(4),
                     lambda blk, b: out_stage(b, 512, 0, o_nak[blk // 2, i, (blk % 2) * 128:(blk % 2 + 1) * 128, :]))
            wv_q, wk_q = piece(w_in_ab[i, :, 0:512], 8, 512)
            tok_proj(wv_u, wk_u, 512, range(4),
                     lambda blk, b: copy_op(evac_eng(), TM[:, blk, :], ps[b][:], [PS(b)], [("tm", blk)]))

            def mixb_tail(b, g, c, pi):
                w = cw(c)
                copy_op("act", pooled[pi][:, 0:w], ps[b][:, 0:w], [PS(b)], [("pooled", pi)])
                b2 = nextbank()
                mm(ps[b2][:, 0:w], wpool[:, g, :], pooled[pi][:, 0:w], True, True, ["wpool", ("pooled", pi)], PS(b2))
                ACT(qkT[:, 8 + g, csl(c)], ps[b2][:, 0:w], AF.Identity, [PS(b2), "psc"], [("qk", 8 + g, c)],
                    scale=psc[:, i, g:g + 1])
            for g in range(4):
                b = nextbank()
                for tb in range(4):
                    blk = tb
                    first, last = (blk % 2 == 0), (blk % 2 == 1)
                    srcs = []
                    if not first:
                        srcs.append((blk - 1, 3))
                    srcs.append((blk, 0 if first else 2))
                    if not last:
                        srcs.append((blk + 1, 4))
                    for n_, (sblk, typ) in enumerate(srcs):
                        mm(ps[b][:, tb * 128:(tb + 1) * 128], TM[:, sblk, g * 128:(g + 1) * 128], poolA[:, g, typ, :],
                           n_ == 0, n_ == len(srcs) - 1, [("tm", sblk), "poolA"], PS(b))
                mixb_tail(b, g, 0, g % 2)

            def v_block(blk, b):
                eng = evac_eng()
                copy_op(eng, TM[:, blk, :], ps[b][:], [PS(b)], [("tm", blk)])
                out_stage(b, 512, 0, o_nav[blk // 2, i, (blk % 2) * 128:(blk % 2 + 1) * 128, :], eng)
            tok_proj(wv_v, wk_v, 512, range(4), v_block)
            exchange_recv(l, True)
            for c in (0, 1):
                for m in range(4):
                    b = std_group(wv_q, wk_q, m, c)
                    copy_op(evac_eng(), qkT[:, m, csl(c)], ps[b][:, 0:cw(c)], [PS(b)], [("qk", m, c)])
            run_attention_units(prompt_units(4, lambda jp: 4 + jp, lambda jp: slice(jp * 128, (jp + 1) * 128)))
            for g in range(4):
                load_cast(poolAs[:].rearrange("p s t -> p (s t)"), c_poolAs[g], "poolAs")
                b = nextbank()
                for s_ in range(8):
                    mm(ps[b][:, 0:256], Us[:, s_, g * 128:(g + 1) * 128], poolAs[:, s_, :], s_ == 0, s_ == 7,
                       [("Us", s_ // 2), "poolAs"], PS(b))
                mixb_tail(b, g, 1, g % 2)

            def prep(jp, hh):
                load_cast(T2s[:], nab[i, 2 * jp + hh], "T2s")
                STT(T2[:, hh, :], T2s[:], 8.0, cmask[:], ALU.mult, ALU.add, ["T2s", "cmask"], [("T2", hh)])
            run_attention_units(sample_units(4, lambda jp: jp, lambda jp: slice(jp * 128, (jp + 1) * 128),
                                             lambda jp: jp, lambda jp: slice(jp * 128, (jp + 1) * 128),
                                             lambda jp, e_, m: T2[:, e_, m * 256:(m + 1) * 256], lambda jp, e_: [("T2", e_)],
                                             prep_fn=prep))

            def src_fn(k, c):
                if k < 4:
                    return HM[:, k, csl(c)], ("hm", k, c)
                return qkT[:, 4 + k, csl(c)], ("qk", 4 + k, c)
            out_proj(l, w_out_ab[i], 16, src_fn)

        def mixer_c(l, i):
            wsrc = w_in_cx[i]

            def rope_groups(wv, wk, wp, wpk, pc):
                for m in range(4):
                    ba = std_group(wv, wk, m, 1)
                    bb = std_group(wp, wpk, m, 1)
                    t1 = next_tmp()
                    t2 = next_tmp()
                    TT(tmp[t1][:, 0:256], ps[ba][:, 0:256], ropeC[:], ALU.mult, [PS(ba), "ropeC"], [("tmp", t1)])
                    TT(tmp[t2][:, 0:256], ps[bb][:, 0:256], ropeS[:], ALU.mult, [PS(bb), "ropeS"], [("tmp", t2)])
                    if pc == 2:
                        TT(SL[:, m * 256:(m + 1) * 256], tmp[t1][:, 0:256], tmp[t2][:, 0:256], ALU.add,
                           [("tmp", t1), ("tmp", t2)], ["SLk"])
                    else:
                        TT(qkT[:, pc * 4 + m, csl(1)], tmp[t1][:, 0:256], tmp[t2][:, 0:256], ALU.add,
                           [("tmp", t1), ("tmp", t2)], [("qk", pc * 4 + m, 1)])

            def plain_groups(wv, wk, pc):
                for m in range(4):
                    b = std_group(wv, wk, m, 0)
                    copy_op(evac_eng(), qkT[:, pc * 4 + m, csl(0)], ps[b][:], [PS(b)], [("qk", pc * 4 + m, 0)])

            def kv_block(blk, b):
                eng = evac_eng()
                vsrc = ps[b][:, 256:512].rearrange("p (g c) -> p g c", g=4)
                if blk < 4:
                    for d_ in range(2):
                        copy_op(eng, TM[:, blk, :].rearrange("p (g d c) -> p g d c", g=4, d=2)[:, :, d_, :], vsrc, [PS(b)], [("tm", blk)])
                    out_stage(b, 256, 0, o_nck[blk // 2, i, (blk % 2) * 128:(blk % 2 + 1) * 128, :], eng)
                    out_stage(b, 256, 256, o_ncv[blk // 2, i, (blk % 2) * 128:(blk % 2 + 1) * 128, :], eng)
                else:
                    for d_ in range(2):
                        copy_op(eng, SL[:, 1024 + (blk - 4) * 512:1024 + (blk - 3) * 512].rearrange("p (g d c) -> p g d c", g=4, d=2)[:, :, d_, :],
                                vsrc, [PS(b)], ["SLv"])
            wv2, wk2 = piece(wsrc[:, 1024:1536], 8, 512)
            wp2, wpk2 = piece(wsrc[:, 2560:3072], 8, 512)
            rope_groups(wv2, wk2, wp2, wpk2, 2)
            wvkv, wkkv = piece(wsrc[:, 3072:3584], 8, 512)
            tok_proj(wvkv, wkkv, 512, (4, 5), kv_block)
            exchange(l, False)
            plain_groups(wv2, wk2, 2)
            qp = [(piece(wsrc[:, 0:512], 8, 512), piece(wsrc[:, 1536:2048], 8, 512))]
            tok_proj(wvkv, wkkv, 512, (0, 1, 2, 3), kv_block)
            exchange_recv(l, False)
            for pc in range(2):
                if pc == 1:
                    qp.append((piece(wsrc[:, 512:1024], 8, 512), piece(wsrc[:, 2048:2560], 8, 512)))
                (wv, wk), (wp, wpk) = qp[pc]
                rope_groups(wv, wk, wp, wpk, pc)
                plain_groups(wv, wk, pc)
            gcols = lambda jp: slice((jp // 2) * 128, (jp // 2 + 1) * 128)
            run_attention_units(
                prompt_units(8, lambda jp: 8 + jp // 2, gcols, sink_fn=lambda jp: sinkE[:, i, jp:jp + 1])
                + sample_units(8, lambda jp: jp // 2, gcols, lambda jp: jp // 2, gcols,
                               lambda jp, e_, m: wmask[:, m, :], lambda jp, e_: ["wmask"],
                               sink_fn=lambda jp: sinkE[:, i, jp:jp + 1]))
            out_proj(l, w_out_c[i], 16, lambda k, c: (HM[:, k, csl(c)], ("hm", k, c)))

        def prefetch_ctx(l):
            if l >= depth:
                return
            i = l // 2
            if l % 2 == 0:
                load_ctx(cak[i], cav[i], 512)
                load_cast(wpool[:], w_pool[i].rearrange("g c d -> c g d"), "wpool")
            else:
                load_ctx(cck[i], ccv[i], 512)

        mod_prologue()
        late_consts()
        NBLK = NT // 128
        for blk in range(NBLK):
            si = next_stg()
            c = 0 if blk < 4 else 1
            load_f32(stg[si][:], xin[blk * 128:(blk + 1) * 128, :], ("stg", si))
            for half in range(2):
                b = nextbank()
                for kk in range(4):
                    k = half * 4 + kk
                    TR(ps[b][:, kk * 128:(kk + 1) * 128], stg[si][:, k * 128:(k + 1) * 128], [("stg", si)], [PS(b)])
                copy_op(evac_eng(), xT[:, half * 4:(half + 1) * 4, blk * 128:(blk + 1) * 128],
                        ps[b][:].rearrange("p (k t) -> p k t", k=4),
                        [PS(b)], [("x", k, c) for k in range(half * 4, half * 4 + 4)])

        prefetch_ctx(0)
        mod_prologue_recv()
        for l in range(depth):
            lastl = (l == depth - 1)
            st["reserved"] = {6, 7}
            norm_phase(l, 0, 0, pre=(l > 0))
            st["reserved"] = set()
            if l % 2 == 0:
                mixer_ab(l, l // 2)
            else:
                mixer_c(l, l // 2)
            if lastl and stop == "mixer":
                break
            norm_phase(l, 1, 24, pre=True)
            st["reserved"] = set()
            prefetch_ctx(l + 1)
            ffn_phase(l)

        pre_ss = depth > 0 and stop is None
        st["reserved"] = {6, 7}
        rs = {0: rstd[:, 0:512], 1: rden[:, 0:256]}
        rkeys = {0: "rstd", 1: "rden"}
        for c in range(2):
            w = cw(c)
            if not pre_ss:
                for k in range(8):
                    ss_accum(k, c)
            ACT(rt[:, 0:w], ps[SSB[c]][:, 0:w], AF.Ln, [PS(SSB[c])], ["rt"], bias=1024 * EPS, scale=1.0)
            ACT(rs[c], rt[:, 0:w], AF.Exp, ["rt"], [rkeys[c]], scale=-0.5)
        st["reserved"] = set()
        for c in range(2):
            w = cw(c)
            ntb = w // 128
            for k in range(8):
                ti = next_tmp()
                STT(tmp[ti][:, 0:w], xT[:, k, csl(c)], gfin[:, k:k + 1], rs[c], ALU.mult, ALU.mult,
                    [("x", k, c), rkeys[c], "gfin"], [("tmp", ti)])
                for tb in range(ntb):
                    b = 2 * tb + (k // 4)
                    TR(ps[b][:, (k % 4) * 128:(k % 4 + 1) * 128], tmp[ti][:, tb * 128:(tb + 1) * 128], [("tmp", ti)], [PS(b)])
            for tb in range(ntb):
                si = next_stg()
                for hf in range(2):
                    copy_op(evac_eng(), stg[si][:, hf * 512:(hf + 1) * 512], ps[2 * tb + hf][:], [PS(2 * tb + hf)], [("stg", si)])
                if c == 0:
                    dst = o_yp[tb * 128:(tb + 1) * 128, :]
                else:
                    dst = o_ys[tb * 128:(tb + 1) * 128, :]
                DMA("sp", dst, stg[si][:], reads=[("stg", si)])

        S.emit(nc)
    return nc


def _consts_shared():
    ident = np.eye(128, dtype=np.float32)
    poolA = np.zeros((128, 4, 5, 128), np.float32)
    n = 384
    for g, w in enumerate((2, 4, 8, 16)):
        A = _pool_matrix(n, w)
        AT = A.T
        poolA[:, g, 0, :] = AT[0:128, 0:128]
        poolA[:, g, 1, :] = AT[128:256, 128:256]
        poolA[:, g, 2, :] = AT[256:384, 256:384]
        poolA[:, g, 3, :] = AT[0:128, 128:256]
        poolA[:, g, 4, :] = AT[256:384, 128:256]
    return dict(c_ident=ident, c_poolA=poolA.reshape(128, -1))


def _pool_matrix(n, w):
    A = np.zeros((n, n), np.float64)
    for tt in range(n):
        lo = min(max(tt - w // 2, 0), n)
        hi = min(max(tt - w // 2 + w, 0), n)
        A[tt, lo:hi] = 1.0 / (hi - lo)
        A[tt, tt] -= 1.0
    return A


def _consts_quarter(qi):
    cq = np.arange(64)
    qstart = np.clip(cq - 8, 0, 48)
    cp = np.arange(64)
    colvalid = (cp[:, None] >= qstart[None, :]) & (cp[:, None] < qstart[None, :] + 16)
    cm = np.full((2, 64, 8, 4, 64), NEG, np.float32)
    for rl in range(4):
        r = 4 * qi + rl
        R0 = min(max(r - 4, 0), 8)
        for m in range(8):
            for e in range(2):
                kr = 2 * m + e
                if R0 <= kr <= R0 + 7:
                    cm[e, :, m, rl, :] = np.where(colvalid, 0.0, NEG)
    cmask = cm.reshape(128, 2048)
    t = qi * 256 + np.arange(256)
    pos = np.stack([t // 64, t % 64], -1).astype(np.float32)
    inv = (10000.0 ** (-np.arange(16, dtype=np.float32) / 16)).astype(np.float32)
    ang = pos[:, :, None] * inv
    cos = np.cos(ang).astype(np.float32)
    sin = np.sin(ang).astype(np.float32)
    C = np.zeros((64, 256), np.float32)
    Sg = np.zeros((64, 256), np.float32)
    for a in range(2):
        for b in range(2):
            for f in range(16):
                d = a * 32 + b * 16 + f
                C[d] = cos[:, a, f]
                Sg[d] = -sin[:, a, f] if b == 0 else sin[:, a, f]
    ropeC = np.concatenate([C, C], 0)
    ropeS = np.concatenate([Sg, Sg], 0)
    k = np.arange(128)[:, None]
    q = np.arange(128)[None, :]
    m1 = np.where(k >= q, 0.0, NEG).astype(np.float32)
    m2 = np.where(k <= q, 0.0, NEG).astype(np.float32)
    wm = np.full((128, 8, 2, 128), NEG, np.float32)
    for qb in range(2):
        n = 2 * qi + qb
        for kn in range(8):
            if kn == n:
                wm[:, kn, qb, :] = 0.0
            elif kn == n - 1:
                wm[:, kn, qb, :] = m1
            elif kn == n + 1:
                wm[:, kn, qb, :] = m2
    wmask = wm.reshape(128, 2048)
    pA = np.zeros((4, 128, 8, 256), np.float32)
    for g, w in enumerate((2, 4, 8, 16)):
        AT = _pool_matrix(1024, w).T
        blk = AT[:, qi * 256:(qi + 1) * 256]
        pA[g] = blk.reshape(8, 128, 256).transpose(1, 0, 2)
    return dict(c_cmask=cmask, c_ropeC=ropeC, c_ropeS=ropeS, c_wmask=wmask, c_poolAs=pA.reshape(4, 128, 2048))


def _fm(v):
    v = np.asarray(v, np.float32)
    lead = v.shape[:-1]
    m = v.shape[-1] // 128
    return np.ascontiguousarray(np.moveaxis(v.reshape(lead + (m, 128)), -1, 0))


_NC_CACHE = {}
_DEBUG = {}


def make_in_maps(x_prompt, x_sample, cache_a_k, cache_a_v, cache_c_k, cache_c_v, c, c_ctx,
                 w_mod, b_mod, norm_mix, norm_ffn, w_in_ab, rpb_a, w_pool, pool_scale, w_out_ab,
                 w_in_c, sink_c, w_out_c, w_gate_up, w_down, norm_final):
    f = lambda a: np.ascontiguousarray(np.asarray(a, dtype=np.float32))
    x_prompt, x_sample = f(x_prompt), f(x_sample)
    cache_a_k, cache_a_v, cache_c_k, cache_c_v = f(cache_a_k), f(cache_a_v), f(cache_c_k), f(cache_c_v)
    c, c_ctx = f(c), f(c_ctx)
    w_in_c = f(w_in_c)
    rpb_a = f(rpb_a)
    sink_c = f(sink_c)

    d = np.arange(64)
    partner = (d // 32) * 32 + (1 - (d % 32) // 16) * 16 + (d % 16)
    qcols = np.arange(1024)
    qperm = (qcols // 64) * 64 + partner[qcols % 64]
    kdup = np.concatenate([1024 + g * 64 + d for g in range(4) for _ in range(2)])
    kdup_perm = np.concatenate([1024 + g * 64 + partner for g in range(4) for _ in range(2)])
    kv = np.arange(1024, 1536)
    colidx = np.concatenate([qcols, kdup, qperm, kdup_perm, kv])
    w_in_cx = np.ascontiguousarray(w_in_c[:, :, colidx])

    cp = np.arange(64)[:, None]
    cq = np.arange(64)[None, :]
    dcol = np.clip(cp - cq + 15, 0, 30)
    nabs = []
    for qi in range(4):
        t_ = np.zeros((2, 8, 2, 64, 8, 4, 64), np.float32)
        for e in range(2):
            for m in range(8):
                for rl in range(4):
                    row = min(max(2 * m + e - (4 * qi + rl) + 7, 0), 14)
                    t_[:, :, e, :, m, rl, :] = rpb_a[:, :, row][:, :, dcol]
        nabs.append(t_.reshape(2, 8, 128, 2048))

    shared = dict(
        gmixT=_fm(norm_mix).reshape(128, -1),
        gffnT=_fm(norm_ffn).reshape(128, -1), gfinT=_fm(norm_final).reshape(128, -1),
        w_in_ab=f(w_in_ab), w_out_ab=f(w_out_ab), w_pool=f(w_pool), pscT=_fm(pool_scale).reshape(128, -1),
        w_in_cx=w_in_cx, w_out_c=f(w_out_c),
        sinkT=np.ascontiguousarray(np.stack([sink_c[:, 0::2], sink_c[:, 1::2]], 0).repeat(64, axis=0).reshape(128, 16)),
        w_gu=f(w_gate_up), w_dn=f(w_down), **_consts_shared())
    qconst = [_consts_quarter(qi) for qi in range(4)]
    w_mod = f(w_mod)
    b_mod = f(b_mod)
    wmq = [np.ascontiguousarray(w_mod[:, :, qi * 1536:(qi + 1) * 1536]) for qi in range(4)]
    bmq = [_fm(b_mod[:, qi * 1536:(qi + 1) * 1536]).reshape(128, 48) for qi in range(4)]
    in_maps = []
    for core in range(8):
        b = core // 4
        qi = core % 4
        xin = np.concatenate([x_prompt[2 * core].reshape(256, 1024), x_prompt[2 * core + 1].reshape(256, 1024),
                              x_sample[b, qi * 256:(qi + 1) * 256]], 0)
        cvec = np.stack([c_ctx, c[b]], 0)
        cvT = np.ascontiguousarray(_fm(cvec).transpose(0, 2, 1)).reshape(128, 16)
        cck = np.ascontiguousarray(cache_c_k[b].reshape(2, 512, 4, 1, 64).repeat(2, axis=3).reshape(2, 512, 512))
        m = dict(shared)
        m.update(qconst[qi])
        m.update(xin=np.ascontiguousarray(xin), cvT=cvT, nab=nabs[qi], w_mod=wmq[qi], bmodT=bmq[qi],
                 cak=np.ascontiguousarray(cache_a_k[b].reshape(2, 512, 512)),
                 cav=np.ascontiguousarray(cache_a_v[b].reshape(2, 512, 512)),
                 cck=cck, ccv=np.ascontiguousarray(cache_c_v[b].reshape(2, 512, 4, 1, 64).repeat(2, axis=3).reshape(2, 512, 512)))
        in_maps.append(m)
    return in_maps


def kernel(**inputs):
    if "nc" not in _NC_CACHE:
        _NC_CACHE["nc"] = build_program(*_DEBUG.get("args", ()))
    nc = _NC_CACHE["nc"]
    in_maps = make_in_maps(**inputs)
    res = run_bass_kernel_spmd(nc, in_maps, core_ids=list(range(8)))
    R = res.results
    y_prompt = np.concatenate([R[i]["o_yp"].reshape(2, 256, 1024) for i in range(8)], 0)
    y_sample = np.concatenate([R[i]["o_ys"] for i in range(8)], 0).reshape(2, 1024, 1024)
    nak = np.concatenate([R[i]["o_nak"] for i in range(8)], 0).reshape(16, 2, 256, 8, 64)
    nav = np.concatenate([R[i]["o_nav"] for i in range(8)], 0).reshape(16, 2, 256, 8, 64)
    nck = np.concatenate([R[i]["o_nck"] for i in range(8)], 0).reshape(16, 2, 256, 4, 64)
    ncv = np.concatenate([R[i]["o_ncv"] for i in range(8)], 0).reshape(16, 2, 256, 4, 64)
    return (y_prompt.astype(np.float32), y_sample.astype(np.float32), nak.astype(np.float32),
            nav.astype(np.float32), nck.astype(np.float32), ncv.astype(np.float32))
```

```python
from contextlib import ExitStack
import numpy as np
import concourse.bass as bass
import concourse.mybir as mybir
from concourse.bass_utils import run_bass_kernel_spmd

F32 = mybir.dt.float32
BF16 = mybir.dt.bfloat16
AF = mybir.ActivationFunctionType
ALU = mybir.AluOpType

ENGS = ("pe", "act", "dve", "pool", "sp")
DMA_CH = 6
NEG = -30000.0
NSLOT = 6
EPS = 1e-6
NT = 768
NP = 512
FFN = 2816


class _Op:
    __slots__ = ("eng", "fn", "reads", "writes", "dma", "deps", "idx", "sig", "chan", "chan_cnt")


class Sched:
    def __init__(self):
        self.ops = []
        self.last_w = {}
        self.readers = {}
        self.ndma = {e: 0 for e in ENGS}
        self.ncc = 0
        self.cc_inc = 1

    def add(self, eng, fn, reads=(), writes=(), dma=False):
        o = _Op()
        o.eng, o.fn, o.dma = eng, fn, dma
        o.reads, o.writes = tuple(reads), tuple(writes)
        o.idx = len(self.ops)
        o.sig = None
        o.chan = None
        deps = set()
        for r in o.reads:
            w = self.last_w.get(r)
            if w is not None:
                deps.add(w)
        for w_ in o.writes:
            w = self.last_w.get(w_)
            if w is not None:
                deps.add(w)
            for rd in self.readers.get(w_, ()):
                deps.add(rd)
        for r in o.reads:
            self.readers.setdefault(r, []).append(o.idx)
        for w_ in o.writes:
            self.last_w[w_] = o.idx
            self.readers[w_] = []
        keep = set()
        for d in deps:
            p = self.ops[d]
            if not p.dma and not o.dma and p.eng == o.eng:
                if o.eng == "pe":
                    continue
            keep.add(d)
        o.deps = keep
        if dma == "cc":
            self.ncc += 1
            o.chan = "cc"
            o.chan_cnt = self.cc_inc * self.ncc
        elif dma:
            n = self.ndma[eng]
            self.ndma[eng] = n + 1
            o.chan = n % DMA_CH
            o.chan_cnt = 16 * (n // DMA_CH + 1)
        self.ops.append(o)
        return o

    def emit(self, nc):
        ops = self.ops
        needed = set()
        for o in ops:
            needed |= o.deps
        cnt = {e: 0 for e in ENGS}
        for o in ops:
            if o.dma:
                continue
            if o.idx in needed:
                cnt[o.eng] += 1
                o.sig = cnt[o.eng]
        with ExitStack() as es:
            esem = {e: es.enter_context(nc.semaphore("s_" + e)) for e in ENGS}
            dsem = {}
            for e in ENGS:
                for c in range(min(DMA_CH, self.ndma[e])):
                    dsem[(e, c)] = es.enter_context(nc.semaphore("d_%s%d" % (e, c)))
            if self.ncc:
                dsem[("pool", "cc")] = es.enter_context(nc.semaphore("d_cc"))
            waits = {}
            waited = {e: {} for e in ENGS}
            for o in ops:
                wl = {}
                if o.dma == "cc":
                    if o.chan_cnt > self.cc_inc:
                        wl[("d", o.eng, o.chan)] = o.chan_cnt - self.cc_inc
                elif o.dma and o.chan_cnt > 16:
                    wl[("d", o.eng, o.chan)] = o.chan_cnt - 16
                for d in o.deps:
                    p = ops[d]
                    if p.dma:
                        key, val = ("d", p.eng, p.chan), p.chan_cnt
                    else:
                        key, val = ("e", p.eng), p.sig
                    if wl.get(key, 0) < val:
                        wl[key] = val
                out = []
                for key, val in wl.items():
                    if waited[o.eng].get(key, 0) >= val:
                        continue
                    waited[o.eng][key] = val
                    out.append((key, val))
                waits[o.idx] = out

            def run(engname, eng):
                for o in ops:
                    if o.eng != engname:
                        continue
                    for key, val in waits[o.idx]:
                        s = esem[key[1]] if key[0] == "e" else dsem[(key[1], key[2])]
                        eng.wait_ge(s, val)
                    ins = o.fn(eng)
                    if o.dma == "cc":
                        ins.then_inc(dsem[(o.eng, o.chan)], self.cc_inc)
                    elif o.dma:
                        ins.then_inc(dsem[(o.eng, o.chan)], 16)
                    elif o.sig is not None:
                        ins.then_inc(esem[o.eng], 1)
                if engname == "pool" and self.ncc:
                    eng.wait_ge(dsem[("pool", "cc")], self.cc_inc * self.ncc)
                n = self.ndma[engname]
                for c in range(min(DMA_CH, n)):
                    k = (n - 1 - c) // DMA_CH + 1
                    eng.wait_ge(dsem[(engname, c)], 16 * k)

            with nc.Block() as block:
                @block.tensor
                def _(eng):
                    run("pe", eng)

                @block.scalar
                def _(eng):
                    run("act", eng)

                @block.vector
                def _(eng):
                    run("dve", eng)

                @block.gpsimd
                def _(eng):
                    run("pool", eng)

                @block.sync
                def _(eng):
                    run("sp", eng)


def build_program(depth=4, stop=None):
    nc = bass.Bass("TRN2", target_bir_lowering=False)

    def din(name, shape):
        return nc.dram_tensor(name, list(shape), F32, kind="ExternalInput").ap()

    def dout(name, shape):
        return nc.dram_tensor(name, list(shape), F32, kind="ExternalOutput").ap()

    xin = din("xin", [NT, 1024])
    cvT = din("cvT", [128, 16])
    w_mod = din("w_mod", [4, 1024, 1536])
    bmodT = din("bmodT", [128, 48])
    gmixT = din("gmixT", [128, 32])
    gffnT = din("gffnT", [128, 32])
    gfinT = din("gfinT", [128, 8])
    w_in_ab = din("w_in_ab", [2, 1024, 2048])
    w_out_ab = din("w_out_ab", [2, 1024, 1024])
    w_pool = din("w_pool", [2, 4, 128, 128])
    pscT = din("pscT", [128, 8])
    w_in_cx = din("w_in_cx", [2, 1024, 3584])
    w_out_c = din("w_out_c", [2, 1024, 1024])
    sinkT = din("sinkT", [128, 16])
    w_gu = din("w_gu", [4, 1024, 2 * FFN])
    w_dn = din("w_dn", [4, FFN, 1024])
    cak = din("cak", [2, 512, 512])
    cav = din("cav", [2, 512, 512])
    cck = din("cck", [2, 512, 512])
    ccv = din("ccv", [2, 512, 512])
    nab = din("nab", [2, 8, 128, 2048])
    c_ident = din("c_ident", [128, 128])
    c_cmask = din("c_cmask", [128, 2048])
    c_ropeC = din("c_ropeC", [128, 256])
    c_ropeS = din("c_ropeS", [128, 256])
    c_wmask = din("c_wmask", [128, 2048])
    c_poolA = din("c_poolA", [128, 4 * 5 * 128])
    c_poolAs = din("c_poolAs", [4, 128, 2048])

    o_yp = dout("o_yp", [512, 1024])
    o_ys = dout("o_ys", [256, 1024])
    o_nak = dout("o_nak", [2, 2, 256, 512])
    o_nav = dout("o_nav", [2, 2, 256, 512])
    o_nck = dout("o_nck", [2, 2, 256, 256])
    o_ncv = dout("o_ncv", [2, 2, 256, 256])

    SLW = [3072, 2048, 3072, 2048]
    slab = [nc.dram_tensor("slab%d" % l, [128, SLW[l]], BF16) for l in range(4)]
    gath = [nc.dram_tensor("gath%d" % l, [512, SLW[l]], BF16) for l in range(4)]
    RG = [[0, 1, 2, 3], [4, 5, 6, 7]]
    mslab0 = nc.dram_tensor("mslab0", [128, 24], F32)
    mgath0 = nc.dram_tensor("mgath0", [512, 24], F32)
    mslab1 = nc.dram_tensor("mslab1", [128, 72], F32)
    mgath1 = nc.dram_tensor("mgath1", [512, 72], F32)

    CH = [(0, 512), (512, 256)]

    S = Sched()
    with ExitStack() as es:
        def sb(name, shape, dt):
            return es.enter_context(nc.sbuf_tensor(name, list(shape), dt))

        xT = sb("xT", [128, 8, NT], F32)
        HM = sb("HM", [128, 8, NT], BF16)
        U1 = sb("U1", [128, 22 * NT], BF16)
        qkT = U1[:, 0:22 * NT].rearrange("p (a t) -> p a t", a=22)
        TM = sb("TM", [128, 4, 512], BF16)
        ring = [sb("ring%d" % i, [128, 2048], BF16) for i in range(NSLOT)]
        KsT = sb("KsT", [128, 4, 1024], BF16)
        Vs = sb("Vs", [128, 8, 512], BF16)
        Us = sb("Us", [128, 8, 512], BF16)
        SL = sb("SL", [128, 3072], BF16)
        Eb = [sb("E%d" % i, [128, 1024], BF16) for i in range(2)]
        T2 = sb("T2", [128, 2, 2048], BF16)
        T2s = sb("T2s", [128, 2048], BF16)
        qz = [sb("qz%d" % i_, [128, 2, 256], BF16) for i_ in range(2)]
        cmask = sb("cmask", [128, 2048], BF16)
        wmask = sb("wmask", [128, 8, 256], BF16)
        poolAs = sb("poolAs", [128, 8, 256], BF16)
        ctxKT = sb("ctxKT", [128, 4, 512], BF16)
        ctxV = sb("ctxV", [128, 4, 512], BF16)
        ropeC = sb("ropeC", [128, 256], F32)
        ropeS = sb("ropeS", [128, 256], F32)
        stg = [sb("stg%d" % i, [128, 1024], F32) for i in range(3)]
        sq = [sb("sq%d" % i, [128, 512], BF16) for i in range(2)]
        tmp = [sb("tmp%d" % i, [128, 512], F32) for i in range(3)]
        rt = sb("rt", [128, 512], F32)
        rstd = sb("rstd", [128, 512], F32)
        rden = sb("rden", [128, 512], F32)
        pooled = [sb("pooled%d" % i, [128, 512], BF16) for i in range(2)]
        ident = sb("ident", [128, 128], F32)
        identb = sb("identb", [128, 128], BF16)
        ones = sb("ones", [128, 128], BF16)
        poolA = sb("poolA", [128, 4, 5, 128], BF16)
        wpool = sb("wpool", [128, 4, 128], BF16)
        modT = sb("modT", [128, 4, 48, 2], F32)
        modP = sb("modP", [128, 48, 2], F32)
        Aab = sb("Aab", [128, 4, 2, 8, 2], F32)
        bmod = sb("bmod", [128, 48], F32)
        gmix = sb("gmix", [128, 4, 8], F32)
        gffn = sb("gffn", [128, 4, 8], F32)
        gfin = sb("gfin", [128, 8], F32)
        psc = sb("psc", [128, 2, 4], F32)
        sinkE = sb("sinkE", [128, 2, 8], F32)
        cv = sb("cv", [128, 8, 2], F32)
        cvb = sb("cvb", [128, 8, 2], BF16)
        ps = [es.enter_context(nc.psum_tensor("ps%d" % i, [128, 512], F32)) for i in range(8)]

        st = {"ring": 0, "bank": 0, "alt": 0, "tmp": 0, "sq": 0, "stg": 0, "reserved": set()}

        def PS(b):
            return ("ps", b)

        def nextbank():
            while True:
                b = st["bank"]
                st["bank"] = (b + 1) % 8
                if b not in st["reserved"]:
                    return b

        def evac_eng():
            st["alt"] ^= 1
            return "act" if st["alt"] else "dve"

        def ACT(out, in_, func, reads, writes, bias=None, scale=None):
            kw = {}
            if bias is not None:
                kw["bias"] = bias
            if scale is not None:
                kw["scale"] = scale
            S.add("act", lambda e: e.activation(out, in_, func, **kw), reads=reads, writes=writes)

        def TT(out, a, b_, op, reads, writes):
            S.add("dve", lambda e: e.tensor_tensor(out, a, b_, op), reads=reads, writes=writes)

        def TS(out, a, s1, op0, reads, writes):
            S.add("dve", lambda e: e.tensor_scalar(out, a, s1, None, op0), reads=reads, writes=writes)

        def STT(out, in0, scalar, in1, op0, op1, reads, writes):
            S.add("dve", lambda e: e.scalar_tensor_tensor(out, in0, scalar, in1, op0, op1), reads=reads, writes=writes)

        def RECIP(out, in_, reads, writes):
            S.add("dve", lambda e: e.reciprocal(out, in_), reads=reads, writes=writes)

        def TR(out, in_, reads, writes):
            S.add("pe", lambda e: e.transpose(out, in_, ident[:]), reads=list(reads) + ["ident"], writes=writes)

        def DMA(q, out, in_, reads=(), writes=()):
            S.add(q, lambda e: e.dma_start(out=out, in_=in_), reads=reads, writes=writes, dma=True)

        def copy_op(eng, out, in_, reads, writes):
            if eng == "act":
                ACT(out, in_, AF.Identity, reads, writes)
            else:
                S.add("dve", lambda e: e.tensor_copy(out, in_), reads=reads, writes=writes)

        def mm(out, lhsT, rhs, start, stop, reads, pskey):
            rr = []
            for r_ in reads:
                if hasattr(r_, "keys") and not isinstance(r_, (tuple, str)):
                    rr.extend(r_.keys)
                else:
                    rr.append(r_)
            S.add("pe", lambda e: e.matmul(out, lhsT, rhs, start=start, stop=stop), reads=rr, writes=[pskey])

        class Piece:
            def __init__(self, views, keys):
                self.views, self.keys = views, keys

            def __getitem__(self, idx):
                p, k, cs = idx
                a_, b_ = cs.start, cs.stop
                h = a_ // 256
                assert (b_ - 1) // 256 == h
                return self.views[h][:, k, a_ - h * 256:b_ - h * 256]

        class PieceKeys:
            def __init__(self, keys):
                self.keys = keys

        def piece(src2d, KT, NC):
            assert KT <= 8
            views, keys = [], []
            for h in range((NC + 255) // 256):
                ncol = min(256, NC - h * 256)
                s_ = st["ring"]
                st["ring"] = (s_ + 1) % NSLOT
                view = ring[s_][:, 0:KT * ncol].rearrange("p (k n) -> p k n", k=KT)
                DMA("pool", view, src2d[:, h * 256:h * 256 + ncol].rearrange("(k p) n -> p k n", p=128), writes=[("w", s_)])
                views.append(view)
                keys.append(("w", s_))
            return Piece(views, keys), PieceKeys(keys)

        def load_f32(dst, src, key):
            DMA("sp", dst, src, writes=[key])

        def load_cast(dst, src, key):
            DMA("pool", dst, src, writes=[key])

        def next_tmp():
            ti = st["tmp"]
            st["tmp"] = (ti + 1) % 3
            return ti

        def next_stg():
            si = st["stg"]
            st["stg"] = (si + 1) % 3
            return si

        def csl(c):
            return slice(CH[c][0], CH[c][0] + CH[c][1])

        def cw(c):
            return CH[c][1]

        load_f32(cv[:].rearrange("p k j -> p (k j)"), cvT, "cv")
        load_f32(bmod[:], bmodT, "bmod")
        load_f32(ident[:], c_ident, "ident")
        load_f32(gmix[:].rearrange("p l k -> p (l k)"), gmixT, "gmix")
        load_f32(gffn[:].rearrange("p l k -> p (l k)"), gffnT, "gffn")
        load_f32(gfin[:], gfinT, "gfin")
        load_f32(psc[:].rearrange("p l g -> p (l g)"), pscT, "psc")
        load_f32(sinkE[:].rearrange("p l g -> p (l g)"), sinkT, "sinkE")
        load_f32(ropeC[:], c_ropeC, "ropeC")
        load_f32(ropeS[:], c_ropeS, "ropeS")
        S.add("dve", lambda e: e.memset(ones[:], 1.0), writes=["ones"])
        for i_ in range(2):
            S.add("dve", lambda e, i_=i_: e.memset(qz[i_][:], 0.0), writes=[("qz", i_, 0), ("qz", i_, 1)])
        ACT(cvb[:].rearrange("p k j -> p (k j)"), cv[:].rearrange("p k j -> p (k j)"), AF.Silu, ["cv"], ["cvb"])
        ACT(sinkE[:].rearrange("p l g -> p (l g)"), sinkE[:].rearrange("p l g -> p (l g)"), AF.Exp, ["sinkE"], ["sinkE"])
        TS(gmix[:].rearrange("p l k -> p (l k)"), gmix[:].rearrange("p l k -> p (l k)"), 32.0, ALU.mult, ["gmix"], ["gmix"])
        TS(gffn[:].rearrange("p l k -> p (l k)"), gffn[:].rearrange("p l k -> p (l k)"), 32.0, ALU.mult, ["gffn"], ["gffn"])
        TS(gfin[:], gfin[:], 32.0, ALU.mult, ["gfin"], ["gfin"])

        def late_consts():
            load_cast(identb[:], c_ident, "identb")
            load_cast(cmask[:], c_cmask, "cmask")
            load_cast(wmask[:].rearrange("p a k -> p (a k)"), c_wmask, "wmask")
            load_cast(poolA[:].rearrange("p g a t -> p (g a t)"), c_poolA, "poolA")

        def aab_layer(l):
            for which, g_, off, nm in ((0, gmix, 8, "gmix"), (1, gffn, 32, "gffn")):
                for j in range(2):
                    STT(Aab[:, l, which, :, j], modT[:, l, off:off + 8, j], 1.0, g_[:, l, :], ALU.add, ALU.mult,
                        [("modT", l), nm], [("Aab", l)])

        def mod_prologue():
            b = nextbank()
            for pc in range(3):
                wv, wk = piece(w_mod[0, :, pc * 512:(pc + 1) * 512], 8, 512)
                for mm_ in range(4):
                    m = pc * 4 + mm_
                    for k in range(8):
                        mm(ps[b][:, m * 2:(m + 1) * 2], wv[:, k, mm_ * 128:(mm_ + 1) * 128], cvb[:, k, :],
                           k == 0, k == 7, [wk, "cvb"], PS(b))
            for j in range(2):
                TT(modP[:, 0:12, j], ps[b][:, 0:24].rearrange("p (m j) -> p m j", j=2)[:, :, j], bmod[:, 0:12], ALU.add,
                   [PS(b), "bmod"], ["modP0"])
            DMA("sp", mslab0.ap(), modP[:, 0:12, :].rearrange("p m j -> p (m j)"), reads=["modP0"], writes=["mslab0"])
            S.add("pool", lambda e: e.collective_compute("AllGather", ALU.bypass, replica_groups=RG,
                                                         ins=[mslab0.ap().opt()], outs=[mgath0.ap().opt()]),
                  reads=["mslab0"], writes=["mgath0"], dma="cc")

        def mod_prologue_recv():
            for r in range(4):
                DMA("sp", modT[:, 0, r * 12:(r + 1) * 12, :],
                    mgath0.ap()[r * 128:(r + 1) * 128, :].rearrange("p (m j) -> p m j", m=12),
                    reads=["mgath0"], writes=[("modT", 0)])
            aab_layer(0)

        mod_todo = [(l, h) for l in range(1, 4) for h in range(6)]

        def mod_step():
            if not mod_todo or depth < 2:
                return
            l, h = mod_todo.pop(0)
            wv, wk = piece(w_mod[l, :, h * 256:(h + 1) * 256], 8, 256)
            b = nextbank()
            for mm_ in range(2):
                for k in range(8):
                    mm(ps[b][:, mm_ * 2:(mm_ + 1) * 2], wv[:, k, mm_ * 128:(mm_ + 1) * 128], cvb[:, k, :],
                       k == 0, k == 7, [wk, "cvb"], PS(b))
            i0 = l * 12 + h * 2
            for j in range(2):
                TT(modP[:, i0:i0 + 2, j], ps[b][:, 0:4].rearrange("p (m j) -> p m j", j=2)[:, :, j], bmod[:, i0:i0 + 2], ALU.add,
                   [PS(b), "bmod"], ["modP1"])

        mod_state = {"finished": False}

        def mod_finish():
            if depth < 2 or mod_state["finished"]:
                return
            mod_state["finished"] = True
            while mod_todo:
                mod_step()
            DMA("sp", mslab1.ap(), modP[:, 12:48, :].rearrange("p m j -> p (m j)"), reads=["modP1"], writes=["mslab1"])
            S.add("pool", lambda e: e.collective_compute("AllGather", ALU.bypass, replica_groups=RG,
                                                         ins=[mslab1.ap().opt()], outs=[mgath1.ap().opt()]),
                  reads=["mslab1"], writes=["mgath1"], dma="cc")

        def mod_finish_recv():
            if depth < 2:
                return
            for r in range(4):
                DMA("sp", modT[:, 1:4, r * 12:(r + 1) * 12, :],
                    mgath1.ap()[r * 128:(r + 1) * 128, :].rearrange("p (l m j) -> p l m j", l=3, m=12),
                    reads=["mgath1"], writes=[("modT", 1), ("modT", 2), ("modT", 3)])
            for l in range(1, 4):
                aab_layer(l)

        SSB = {0: 6, 1: 7}

        def ss_accum(k, c):
            w = cw(c)
            si = st["sq"]
            st["sq"] ^= 1
            ACT(sq[si][:, 0:w], xT[:, k, csl(c)], AF.Square, [("x", k, c)], [("sq", si)])
            mm(ps[SSB[c]][:, 0:w], ones[:], sq[si][:, 0:w], k == 0, k == 7, ["ones", ("sq", si)], PS(SSB[c]))

        ss_pending = []

        def ss_defer(k, c):
            ss_pending.append((k, c))
            if len(ss_pending) > 4:
                ss_accum(*ss_pending.pop(0))

        def ss_flush():
            while ss_pending:
                ss_accum(*ss_pending.pop(0))

        def rstd_finish(c):
            w = cw(c)
            ACT(rt[:, 0:w], ps[SSB[c]][:, 0:w], AF.Ln, [PS(SSB[c])], ["rt"], bias=1024 * EPS, scale=1.0)
            ACT(rstd[:, 0:w], rt[:, 0:w], AF.Exp, ["rt"], ["rstd"], scale=-0.5)

        def rstd_chunk(c):
            for k in range(8):
                ss_accum(k, c)
            rstd_finish(c)

        def norm_phase(l, which, off_b, pre=False):
            for c in range(2):
                j = c
                w = cw(c)
                if pre:
                    rstd_finish(c)
                else:
                    rstd_chunk(c)
                for k in range(8):
                    ti = next_tmp()
                    TT(tmp[ti][:, 0:w], xT[:, k, csl(c)], rstd[:, 0:w], ALU.mult, [("x", k, c), "rstd"], [("tmp", ti)])
                    ACT(HM[:, k, csl(c)], tmp[ti][:, 0:w], AF.Identity, [("tmp", ti), ("modT", l), ("Aab", l)], [("hm", k, c)],
                        bias=modT[:, l, off_b + k, j:j + 1], scale=Aab[:, l, which, k, j:j + 1])

        def std_group(wv, wk, m, c):
            b = nextbank()
            w = cw(c)
            for k in range(8):
                mm(ps[b][:, 0:w], wv[:, k, m * 128:(m + 1) * 128], HM[:, k, csl(c)],
                   k == 0, k == 7, [wk, ("hm", k, c)], PS(b))
            return b

        def tok_proj(wv, wk, ncols, blocks, on_block):
            for blk in blocks:
                c = 0 if blk < 4 else 1
                b = nextbank()
                for h in range(ncols // 256):
                    for k in range(8):
                        mm(ps[b][:, h * 256:(h + 1) * 256], HM[:, k, blk * 128:(blk + 1) * 128], wv[:, k, h * 256:(h + 1) * 256],
                           k == 0, k == 7, [wk, ("hm", k, c)], PS(b))
                on_block(blk, b)

        def out_stage(b, ncols, col0, dst_dram, eng=None):
            si = next_stg()
            copy_op(eng or evac_eng(), stg[si][:, 0:ncols], ps[b][:, col0:col0 + ncols], [PS(b)], [("stg", si)])
            DMA("sp", dst_dram, stg[si][:, 0:ncols], reads=[("stg", si)])

        att = {"S": 0, "OD": 0, "qz": 0}

        def attend_A(qz_ap, q_reads, N, kblocks):
            sbuf_i = att["S"]
            att["S"] ^= 1
            banks = (0, 1) if sbuf_i == 0 else (2, 3)
            nkb = len(kblocks)
            assert nkb * N <= 1024
            per_bank = 512 // N
            for i, kb in enumerate(kblocks):
                bk = banks[i // per_bank]
                col = (i % per_bank) * N
                has_bias = kb[4] is not None
                mm(ps[bk][:, col:col + N], kb[0], qz_ap, True, not has_bias, list(kb[1]) + list(q_reads), PS(bk))
                if has_bias:
                    mm(ps[bk][:, col:col + N], identb[:], kb[4], False, True, ["identb"] + list(kb[5]), PS(bk))
            E = Eb[sbuf_i]
            nb_used = (nkb + per_bank - 1) // per_bank
            for bi in range(nb_used):
                ncol = min(nkb - bi * per_bank, per_bank) * N
                bk = banks[bi]
                ACT(E[:, bi * 512:bi * 512 + ncol], ps[bk][:, 0:ncol], AF.Exp, [PS(bk)], [("E", sbuf_i, bi)], scale=0.125)
            return sbuf_i

        def attend_B(sbuf_i, N, kblocks, obank, dbank, ocol, acc_start, acc_stop):
            E = Eb[sbuf_i]
            nkb = len(kblocks)
            per_bank = 512 // N
            for i, kb in enumerate(kblocks):
                bi = i // per_bank
                col = bi * 512 + (i % per_bank) * N
                mm(ps[obank][:, ocol:ocol + N], kb[2], E[:, col:col + N], acc_start and i == 0, acc_stop and i == nkb - 1,
                   list(kb[3]) + [("E", sbuf_i, bi)], PS(obank))
            for i, kb in enumerate(kblocks):
                bi = i // per_bank
                col = bi * 512 + (i % per_bank) * N
                mm(ps[dbank][:, ocol:ocol + N], ones[:], E[:, col:col + N], acc_start and i == 0, acc_stop and i == nkb - 1,
                   ["ones", ("E", sbuf_i, bi)], PS(dbank))

        def make_qz(q_tile, c0, q_key):
            bi = att["qz"]
            att["qz"] ^= 1
            for e_ in range(2):
                rows = slice(e_ * 64, (e_ + 1) * 64)
                S.add("dve", lambda e, e_=e_, rows=rows: e.tensor_copy(qz[bi][rows, e_, :], qkT[rows, q_tile, c0:c0 + 256]),
                      reads=[q_key], writes=[("qz", bi, e_)])
            return bi

        def od_banks():
            i = att["OD"]
            att["OD"] ^= 1
            return (4, 5) if i == 0 else (6, 7)

        def normalize(obank, dbank, dst_tile, c0, dst_key, sink_ap=None):
            if sink_ap is not None:
                ACT(rden[:], ps[dbank][:], AF.Ln, [PS(dbank), "sinkE"], ["rden"], bias=sink_ap, scale=1.0)
            else:
                ACT(rden[:], ps[dbank][:], AF.Ln, [PS(dbank)], ["rden"])
            ACT(rden[:], rden[:], AF.Exp, ["rden"], ["rden"], scale=-1.0)
            for e_ in range(2):
                rows = slice(e_ * 64, (e_ + 1) * 64)
                cols = slice(e_ * 256, (e_ + 1) * 256)
                TT(HM[rows, dst_tile, c0:c0 + 256], ps[obank][rows, cols], rden[rows, cols], ALU.mult,
                   [PS(obank), "rden"], [dst_key])

        def load_ctx(kd, vd, vcols):
            load_cast(ctxV[:, :, 0:vcols], vd.rearrange("(i p) f -> p i f", p=128), "ctxV")
            for blk in range(4):
                si = next_stg()
                load_f32(stg[si][:, 0:512], kd[blk * 128:(blk + 1) * 128, :], ("stg", si))
                b = nextbank()
                for j in range(4):
                    TR(ps[b][:, j * 128:(j + 1) * 128], stg[si][:, j * 128:(j + 1) * 128], [("stg", si)], [PS(b)])
                copy_op(evac_eng(), ctxKT[:, :, blk * 128:(blk + 1) * 128], ps[b][:].rearrange("p (j t) -> p j t", j=4),
                        [PS(b)], ["ctxKT"])

        def exchange(l, even):
            W = SLW[l]
            DMA("sp", slab[l].ap(), SL[:, 0:W], reads=["SLk", "SLv", "SLu"], writes=[("slab", l)])
            S.add("pool", lambda e: e.collective_compute("AllGather", ALU.bypass, replica_groups=RG,
                                                         ins=[slab[l].ap().opt()], outs=[gath[l].ap().opt()]),
                  reads=[("slab", l)], writes=[("gath", l)], dma="cc")

        def exchange_recv(l, even):
            g = gath[l].ap()
            vw = 512
            for r in range(4):
                rows = slice(r * 128, (r + 1) * 128)
                DMA("sp", KsT[:, :, r * 256:(r + 1) * 256], g[rows, 0:1024].rearrange("p (j t) -> p j t", j=4),
                    reads=[("gath", l)], writes=[("KsT", r)])
                DMA("sp", Vs[:, 2 * r:2 * r + 2, 0:vw], g[rows, 1024:1024 + 2 * vw].rearrange("p (b f) -> p b f", b=2),
                    reads=[("gath", l)], writes=[("Vs", r)])
                if even:
                    DMA("sp", Us[:, 2 * r:2 * r + 2, :], g[rows, 2048:3072].rearrange("p (b f) -> p b f", b=2),
                        reads=[("gath", l)], writes=[("Us", r)])

        def run_attention_units(units):
            nu = len(units)
            obs = [od_banks() for _ in units]
            qzb = [None] * nu
            calls = []
            for ui, u in enumerate(units):
                for e_ in range(2):
                    kbl = u["kbl_fn"](e_)
                    np_ = u["parts"]
                    per = len(kbl) // np_
                    for part in range(np_):
                        calls.append(dict(ui=ui, e=e_, kbl=kbl[part * per:(part + 1) * per], start=(part == 0), stop=(part == np_ - 1),
                                          last_of_head=(part == np_ - 1), last_of_unit=(e_ == 1 and part == np_ - 1)))

            def ensure_unit_ready(ui):
                if ui < nu and qzb[ui] is None:
                    u = units[ui]
                    qzb[ui] = make_qz(u["q_tile"], u["c0"], u["q_key"])

            def prep_head(ui, e_):
                if ui < nu and units[ui].get("prep") is not None:
                    units[ui]["prep"](e_)
            ensure_unit_ready(0)
            prep_head(0, 0)
            prep_head(0, 1)
            pending = None
            for c_ in calls:
                ui = c_["ui"]
                u = units[ui]
                ensure_unit_ready(ui)
                si = attend_A(qz[qzb[ui]][:, c_["e"], :], [("qz", qzb[ui], c_["e"])], 256, c_["kbl"])
                if c_["last_of_head"]:
                    prep_head(ui + 1, c_["e"])
                if c_["e"] == 0 and c_["start"]:
                    ensure_unit_ready(ui + 1)
                if pending is not None:
                    pc_, psi = pending
                    pob, pdb = obs[pc_["ui"]]
                    attend_B(psi, 256, pc_["kbl"], pob, pdb, pc_["e"] * 256, pc_["start"], pc_["stop"])
                    if pc_["last_of_unit"]:
                        pu = units[pc_["ui"]]
                        normalize(pob, pdb, pu["dst_tile"], pu["c0"], pu["dst_key"], sink_ap=pu.get("sink"))
                pending = (c_, si)
            pc_, psi = pending
            pob, pdb = obs[pc_["ui"]]
            attend_B(psi, 256, pc_["kbl"], pob, pdb, pc_["e"] * 256, pc_["start"], pc_["stop"])
            pu = units[pc_["ui"]]
            normalize(pob, pdb, pu["dst_tile"], pu["c0"], pu["dst_key"], sink_ap=pu.get("sink"))

        def sample_units(nheadpairs, kt_tile_fn, v_col_fn, ctx_tile_fn, ctxv_col_fn, table_fn, table_keys_fn,
                         sink_fn=None, prep_fn=None):
            units = []
            for jp in range(nheadpairs):
                def kbl_fn(e_, jp=jp):
                    kbl = []
                    for ib in range(4):
                        kbl.append((ctxKT[:, ctx_tile_fn(jp), ib * 128:(ib + 1) * 128], ["ctxKT"],
                                    ctxV[:, ib, ctxv_col_fn(jp)], ["ctxV"], None, None))
                    for m in range(8):
                        kbl.append((KsT[:, kt_tile_fn(jp), m * 128:(m + 1) * 128], [("KsT", m // 2)],
                                    Vs[:, m, v_col_fn(jp)], [("Vs", m // 2)],
                                    table_fn(jp, e_, m), table_keys_fn(jp, e_)))
                    return kbl
                units.append(dict(q_tile=jp, c0=CH[1][0], q_key=("qk", jp, 1), kbl_fn=kbl_fn, parts=3,
                                  dst_tile=jp, dst_key=("hm", jp, 1),
                                  sink=None if sink_fn is None else sink_fn(jp),
                                  prep=None if prep_fn is None else (lambda e_, jp=jp: prep_fn(jp, e_))))
            return units

        def prompt_units(nheadpairs, kt_tile_fn, v_col_fn, sink_fn=None):
            units = []
            for jp in range(nheadpairs):
                for s_ in range(2):
                    def kbl_fn(e_, jp=jp, s_=s_):
                        kbl = []
                        for kb in range(2):
                            blk = s_ * 2 + kb
                            kbl.append((qkT[:, kt_tile_fn(jp), blk * 128:(blk + 1) * 128], [("qk", kt_tile_fn(jp), 0)],
                                        TM[:, blk, v_col_fn(jp)], [("tm", blk)], None, None))
                        return kbl
                    units.append(dict(q_tile=jp, c0=s_ * 256, q_key=("qk", jp, 0), kbl_fn=kbl_fn, parts=1,
                                      dst_tile=jp, dst_key=("hm", jp, 0),
                                      sink=None if sink_fn is None else sink_fn(jp)))
            return units

        def resid_update(bank, m, c, gate_ap, l):
            w = cw(c)
            STT(xT[:, m, csl(c)], ps[bank][:, 0:w], gate_ap, xT[:, m, csl(c)], ALU.mult, ALU.add,
                [PS(bank), ("x", m, c), ("modT", l)], [("x", m, c)])

        def out_proj(l, wsrc, gate_off, src_fn):
            st["reserved"] = {6, 7}
            for pc in range(2):
                wv, wk = piece(wsrc[:, pc * 512:(pc + 1) * 512], 8, 512)
                for c in range(2):
                    w = cw(c)
                    for mm_ in range(4):
                        m = pc * 4 + mm_
                        b = nextbank()
                        for k in range(8):
                            ap_, key_ = src_fn(k, c)
                            mm(ps[b][:, 0:w], wv[:, k, mm_ * 128:(mm_ + 1) * 128], ap_, k == 0, k == 7, [wk, key_], PS(b))
                        resid_update(b, m, c, modT[:, l, gate_off + m, c:c + 1], l)
                        ss_defer(m, c)
            ss_flush()

        def ffn_phase(l):
            hid = qkT
            for t0 in range(0, 22, 2):
                gcol = t0 * 128
                wg, wgk = piece(w_gu[l, :, gcol:gcol + 256], 8, 256)
                wu, wuk = piece(w_gu[l, :, FFN + gcol:FFN + gcol + 256], 8, 256)
                for c in range(2):
                    w = cw(c)
                    for t in range(2):
                        bg = std_group(wg, wgk, t, c)
                        bu = std_group(wu, wuk, t, c)
                        ti = next_tmp()
                        ACT(tmp[ti][:, 0:w], ps[bg][:, 0:w], AF.Silu, [PS(bg)], [("tmp", ti)])
                        TT(hid[:, t0 + t, csl(c)], ps[bu][:, 0:w], tmp[ti][:, 0:w], ALU.mult,
                           [PS(bu), ("tmp", ti)], [("qk", t0 + t, c)])
                if l == 0:
                    mod_step()
                    if t0 < 14:
                        mod_step()
            if l == 0:
                mod_finish()
            st["reserved"] = {6, 7}
            kb_ = [0, 8, 16, 22]
            for pc in range(4):
                pcs = [piece(w_dn[l, kb_[j_] * 128:kb_[j_ + 1] * 128, pc * 256:(pc + 1) * 256], kb_[j_ + 1] - kb_[j_], 256)
                       for j_ in range(3)]
                for c in range(2):
                    w = cw(c)
                    for mm_ in range(2):
                        m = pc * 2 + mm_
                        b = nextbank()
                        for k in range(22):
                            j_ = max(jj for jj in range(3) if kb_[jj] <= k)
                            wv_, wk_ = pcs[j_]
                            mm(ps[b][:, 0:w], wv_[:, k - kb_[j_], mm_ * 128:(mm_ + 1) * 128], hid[:, k, csl(c)],
                               k == 0, k == 21, [wk_, ("qk", k, c)], PS(b))
                        resid_update(b, m, c, modT[:, l, 40 + m, c:c + 1], l)
                        ss_defer(m, c)
            ss_flush()
            if l == 0:
                mod_finish()
                mod_finish_recv()

        def mixer_ab(l, i):
            wv_k, wk_k = piece(w_in_ab[i, :, 512:1024], 8, 512)
            for m in range(4):
                b = std_group(wv_k, wk_k, m, 1)
                copy_op(evac_eng(), SL[:, m * 256:(m + 1) * 256], ps[b][:, 0:256], [PS(b)], ["SLk"])
            wv_v, wk_v = piece(w_in_ab[i, :, 1024:1536], 8, 512)
            tok_proj(wv_v, wk_v, 512, (4, 5),
                     lambda blk, b: copy_op(evac_eng(), SL[:, 1024 + (blk - 4) * 512:1024 + (blk - 3) * 512], ps[b][:], [PS(b)], ["SLv"]))
            wv_u, wk_u = piece(w_in_ab[i, :, 1536:2048], 8, 512)
            tok_proj(wv_u, wk_u, 512, (4, 5),
                     lambda blk, b: copy_op(evac_eng(), SL[:, 2048 + (blk - 4) * 512:2048 + (blk - 3) * 512], ps[b][:], [PS(b)], ["SLu"]))
            exchange(l, True)
            for m in range(4):
                b = std_group(wv_k, wk_k, m, 0)
                copy_op(evac_eng(), qkT[:, 4 + m, csl(0)], ps[b][:], [PS(b)], [("qk", 4 + m, 0)])
            tok_proj(wv_k, wk_k, 512, range(4),
                     lambda blk, b: out_stage(b, 512, 0, o_nak[blk // 2, i, (blk % 2) * 128:(blk % 2 + 1) * 128, :]))
            wv_q, wk_q = piece(w_in_ab[i, :, 0:512], 8, 512)
            tok_proj(wv_u, wk_u, 512, range(4),
                     lambda blk, b: copy_op(evac_eng(), TM[:, blk, :], ps[b][:], [PS(b)], [("tm", blk)]))

            def mixb_tail(b, g, c, pi):
                w = cw(c)
                copy_op("act", pooled[pi][:, 0:w], ps[b][:, 0:w], [PS(b)], [("pooled", pi)])
                b2 = nextbank()
                mm(ps[b2][:, 0:w], wpool[:, g, :], pooled[pi][:, 0:w], True, True, ["wpool", ("pooled", pi)], PS(b2))
                ACT(qkT[:, 8 + g, csl(c)], ps[b2][:, 0:w], AF.Identity, [PS(b2), "psc"], [("qk", 8 + g, c)],
                    scale=psc[:, i, g:g + 1])
            for g in range(4):
                b = nextbank()
                for tb in range(4):
                    blk = tb
                    first, last = (blk % 2 == 0), (blk % 2 == 1)
                    srcs = []
                    if not first:
                        srcs.append((blk - 1, 3))
                    srcs.append((blk, 0 if first else 2))
                    if not last:
                        srcs.append((blk + 1, 4))
                    for n_, (sblk, typ) in enumerate(srcs):
                        mm(ps[b][:, tb * 128:(tb + 1) * 128], TM[:, sblk, g * 128:(g + 1) * 128], poolA[:, g, typ, :],
                           n_ == 0, n_ == len(srcs) - 1, [("tm", sblk), "poolA"], PS(b))
                mixb_tail(b, g, 0, g % 2)

            def v_block(blk, b):
                eng = evac_eng()
                copy_op(eng, TM[:, blk, :], ps[b][:], [PS(b)], [("tm", blk)])
                out_stage(b, 512, 0, o_nav[blk // 2, i, (blk % 2) * 128:(blk % 2 + 1) * 128, :], eng)
            tok_proj(wv_v, wk_v, 512, range(4), v_block)
            exchange_recv(l, True)
            for c in (0, 1):
                for m in range(4):
                    b = std_group(wv_q, wk_q, m, c)
                    copy_op(evac_eng(), qkT[:, m, csl(c)], ps[b][:, 0:cw(c)], [PS(b)], [("qk", m, c)])
            run_attention_units(prompt_units(4, lambda jp: 4 + jp, lambda jp: slice(jp * 128, (jp + 1) * 128)))
            for g in range(4):
                load_cast(poolAs[:].rearrange("p s t -> p (s t)"), c_poolAs[g], "poolAs")
                b = nextbank()
                for s_ in range(8):
                    mm(ps[b][:, 0:256], Us[:, s_, g * 128:(g + 1) * 128], poolAs[:, s_, :], s_ == 0, s_ == 7,
                       [("Us", s_ // 2), "poolAs"], PS(b))
                mixb_tail(b, g, 1, g % 2)

            def prep(jp, hh):
                load_cast(T2s[:], nab[i, 2 * jp + hh], "T2s")
                STT(T2[:, hh, :], T2s[:], 8.0, cmask[:], ALU.mult, ALU.add, ["T2s", "cmask"], [("T2", hh)])
            run_attention_units(sample_units(4, lambda jp: jp, lambda jp: slice(jp * 128, (jp + 1) * 128),
                                             lambda jp: jp, lambda jp: slice(jp * 128, (jp + 1) * 128),
                                             lambda jp, e_, m: T2[:, e_, m * 256:(m + 1) * 256], lambda jp, e_: [("T2", e_)],
                                             prep_fn=prep))

            def src_fn(k, c):
                if k < 4:
                    return HM[:, k, csl(c)], ("hm", k, c)
                return qkT[:, 4 + k, csl(c)], ("qk", 4 + k, c)
            out_proj(l, w_out_ab[i], 16, src_fn)

        def mixer_c(l, i):
            wsrc = w_in_cx[i]

            def rope_groups(wv, wk, wp, wpk, pc):
                for m in range(4):
                    ba = std_group(wv, wk, m, 1)
                    bb = std_group(wp, wpk, m, 1)
                    t1 = next_tmp()
                    t2 = next_tmp()
                    TT(tmp[t1][:, 0:256], ps[ba][:, 0:256], ropeC[:], ALU.mult, [PS(ba), "ropeC"], [("tmp", t1)])
                    TT(tmp[t2][:, 0:256], ps[bb][:, 0:256], ropeS[:], ALU.mult, [PS(bb), "ropeS"], [("tmp", t2)])
                    if pc == 2:
                        TT(SL[:, m * 256:(m + 1) * 256], tmp[t1][:, 0:256], tmp[t2][:, 0:256], ALU.add,
                           [("tmp", t1), ("tmp", t2)], ["SLk"])
                    else:
                        TT(qkT[:, pc * 4 + m, csl(1)], tmp[t1][:, 0:256], tmp[t2][:, 0:256], ALU.add,
                           [("tmp", t1), ("tmp", t2)], [("qk", pc * 4 + m, 1)])

            def plain_groups(wv, wk, pc):
                for m in range(4):
                    b = std_group(wv, wk, m, 0)
                    copy_op(evac_eng(), qkT[:, pc * 4 + m, csl(0)], ps[b][:], [PS(b)], [("qk", pc * 4 + m, 0)])

            def kv_block(blk, b):
                eng = evac_eng()
                vsrc = ps[b][:, 256:512].rearrange("p (g c) -> p g c", g=4)
                if blk < 4:
                    for d_ in range(2):
                        copy_op(eng, TM[:, blk, :].rearrange("p (g d c) -> p g d c", g=4, d=2)[:, :, d_, :], vsrc, [PS(b)], [("tm", blk)])
                    out_stage(b, 256, 0, o_nck[blk // 2, i, (blk % 2) * 128:(blk % 2 + 1) * 128, :], eng)
                    out_stage(b, 256, 256, o_ncv[blk // 2, i, (blk % 2) * 128:(blk % 2 + 1) * 128, :], eng)
                else:
                    for d_ in range(2):
                        copy_op(eng, SL[:, 1024 + (blk - 4) * 512:1024 + (blk - 3) * 512].rearrange("p (g d c) -> p g d c", g=4, d=2)[:, :, d_, :],
                                vsrc, [PS(b)], ["SLv"])
            wv2, wk2 = piece(wsrc[:, 1024:1536], 8, 512)
            wp2, wpk2 = piece(wsrc[:, 2560:3072], 8, 512)
            rope_groups(wv2, wk2, wp2, wpk2, 2)
            wvkv, wkkv = piece(wsrc[:, 3072:3584], 8, 512)
            tok_proj(wvkv, wkkv, 512, (4, 5), kv_block)
            exchange(l, False)
            plain_groups(wv2, wk2, 2)
            qp = [(piece(wsrc[:, 0:512], 8, 512), piece(wsrc[:, 1536:2048], 8, 512))]
            tok_proj(wvkv, wkkv, 512, (0, 1, 2, 3), kv_block)
            exchange_recv(l, False)
            for pc in range(2):
                if pc == 1:
                    qp.append((piece(wsrc[:, 512:1024], 8, 512), piece(wsrc[:, 2048:2560], 8, 512)))
                (wv, wk), (wp, wpk) = qp[pc]
                rope_groups(wv, wk, wp, wpk, pc)
                plain_groups(wv, wk, pc)
            gcols = lambda jp: slice((jp // 2) * 128, (jp // 2 + 1) * 128)
            run_attention_units(
                prompt_units(8, lambda jp: 8 + jp // 2, gcols, sink_fn=lambda jp: sinkE[:, i, jp:jp + 1])
                + sample_units(8, lambda jp: jp // 2, gcols, lambda jp: jp // 2, gcols,
                               lambda jp, e_, m: wmask[:, m, :], lambda jp, e_: ["wmask"],
                               sink_fn=lambda jp: sinkE[:, i, jp:jp + 1]))
            out_proj(l, w_out_c[i], 16, lambda k, c: (HM[:, k, csl(c)], ("hm", k, c)))

        def prefetch_ctx(l):
            if l >= depth:
                return
            i = l // 2
            if l % 2 == 0:
                load_ctx(cak[i], cav[i], 512)
                load_cast(wpool[:], w_pool[i].rearrange("g c d -> c g d"), "wpool")
            else:
                load_ctx(cck[i], ccv[i], 512)

        mod_prologue()
        late_consts()
        NBLK = NT // 128
        for blk in range(NBLK):
            si = next_stg()
            c = 0 if blk < 4 else 1
            load_f32(stg[si][:], xin[blk * 128:(blk + 1) * 128, :], ("stg", si))
            for half in range(2):
                b = nextbank()
                for kk in range(4):
                    k = half * 4 + kk
                    TR(ps[b][:, kk * 128:(kk + 1) * 128], stg[si][:, k * 128:(k + 1) * 128], [("stg", si)], [PS(b)])
                copy_op(evac_eng(), xT[:, half * 4:(half + 1) * 4, blk * 128:(blk + 1) * 128],
                        ps[b][:].rearrange("p (k t) -> p k t", k=4),
                        [PS(b)], [("x", k, c) for k in range(half * 4, half * 4 + 4)])

        prefetch_ctx(0)
        mod_prologue_recv()
        for l in range(depth):
            lastl = (l == depth - 1)
            st["reserved"] = {6, 7}
            norm_phase(l, 0, 0, pre=(l > 0))
            st["reserved"] = set()
            if l % 2 == 0:
                mixer_ab(l, l // 2)
            else:
                mixer_c(l, l // 2)
            if lastl and stop == "mixer":
                break
            norm_phase(l, 1, 24, pre=True)
            st["reserved"] = set()
            prefetch_ctx(l + 1)
            ffn_phase(l)

        pre_ss = depth > 0 and stop is None
        st["reserved"] = {6, 7}
        rs = {0: rstd[:, 0:512], 1: rden[:, 0:256]}
        rkeys = {0: "rstd", 1: "rden"}
        for c in range(2):
            w = cw(c)
            if not pre_ss:
                for k in range(8):
                    ss_accum(k, c)
            ACT(rt[:, 0:w], ps[SSB[c]][:, 0:w], AF.Ln, [PS(SSB[c])], ["rt"], bias=1024 * EPS, scale=1.0)
            ACT(rs[c], rt[:, 0:w], AF.Exp, ["rt"], [rkeys[c]], scale=-0.5)
        st["reserved"] = set()
        for c in range(2):
            w = cw(c)
            ntb = w // 128
            for k in range(8):
                ti = next_tmp()
                STT(tmp[ti][:, 0:w], xT[:, k, csl(c)], gfin[:, k:k + 1], rs[c], ALU.mult, ALU.mult,
                    [("x", k, c), rkeys[c], "gfin"], [("tmp", ti)])
                for tb in range(ntb):
                    b = 2 * tb + (k // 4)
                    TR(ps[b][:, (k % 4) * 128:(k % 4 + 1) * 128], tmp[ti][:, tb * 128:(tb + 1) * 128], [("tmp", ti)], [PS(b)])
            for tb in range(ntb):
                si = next_stg()
                for hf in range(2):
                    copy_op(evac_eng(), stg[si][:, hf * 512:(hf + 1) * 512], ps[2 * tb + hf][:], [PS(2 * tb + hf)], [("stg", si)])
                if c == 0:
                    dst = o_yp[tb * 128:(tb + 1) * 128, :]
                else:
                    dst = o_ys[tb * 128:(tb + 1) * 128, :]
                DMA("sp", dst, stg[si][:], reads=[("stg", si)])

        S.emit(nc)
    return nc


def _consts_shared():
    ident = np.eye(128, dtype=np.float32)
    poolA = np.zeros((128, 4, 5, 128), np.float32)
    n = 384
    for g, w in enumerate((2, 4, 8, 16)):
        A = _pool_matrix(n, w)
        AT = A.T
        poolA[:, g, 0, :] = AT[0:128, 0:128]
        poolA[:, g, 1, :] = AT[128:256, 128:256]
        poolA[:, g, 2, :] = AT[256:384, 256:384]
        poolA[:, g, 3, :] = AT[0:128, 128:256]
        poolA[:, g, 4, :] = AT[256:384, 128:256]
    return dict(c_ident=ident, c_poolA=poolA.reshape(128, -1))


def _pool_matrix(n, w):
    A = np.zeros((n, n), np.float64)
    for tt in range(n):
        lo = min(max(tt - w // 2, 0), n)
        hi = min(max(tt - w // 2 + w, 0), n)
        A[tt, lo:hi] = 1.0 / (hi - lo)
        A[tt, tt] -= 1.0
    return A


def _consts_quarter(qi):
    cq = np.arange(64)
    qstart = np.clip(cq - 8, 0, 48)
    cp = np.arange(64)
    colvalid = (cp[:, None] >= qstart[None, :]) & (cp[:, None] < qstart[None, :] + 16)
    cm = np.full((2, 64, 8, 4, 64), NEG, np.float32)
    for rl in range(4):
        r = 4 * qi + rl
        R0 = min(max(r - 4, 0), 8)
        for m in range(8):
            for e in range(2):
                kr = 2 * m + e
                if R0 <= kr <= R0 + 7:
                    cm[e, :, m, rl, :] = np.where(colvalid, 0.0, NEG)
    cmask = cm.reshape(128, 2048)
    t = qi * 256 + np.arange(256)
    pos = np.stack([t // 64, t % 64], -1).astype(np.float32)
    inv = (10000.0 ** (-np.arange(16, dtype=np.float32) / 16)).astype(np.float32)
    ang = pos[:, :, None] * inv
    cos = np.cos(ang).astype(np.float32)
    sin = np.sin(ang).astype(np.float32)
    C = np.zeros((64, 256), np.float32)
    Sg = np.zeros((64, 256), np.float32)
    for a in range(2):
        for b in range(2):
            for f in range(16):
                d = a * 32 + b * 16 + f
                C[d] = cos[:, a, f]
                Sg[d] = -sin[:, a, f] if b == 0 else sin[:, a, f]
    ropeC = np.concatenate([C, C], 0)
    ropeS = np.concatenate([Sg, Sg], 0)
    k = np.arange(128)[:, None]
    q = np.arange(128)[None, :]
    m1 = np.where(k >= q, 0.0, NEG).astype(np.float32)
    m2 = np.where(k <= q, 0.0, NEG).astype(np.float32)
    wm = np.full((128, 8, 2, 128), NEG, np.float32)
    for qb in range(2):
        n = 2 * qi + qb
        for kn in range(8):
            if kn == n:
                wm[:, kn, qb, :] = 0.0
            elif kn == n - 1:
                wm[:, kn, qb, :] = m1
            elif kn == n + 1:
                wm[:, kn, qb, :] = m2
    wmask = wm.reshape(128, 2048)
    pA = np.zeros((4, 128, 8, 256), np.float32)
    for g, w in enumerate((2, 4, 8, 16)):
        AT = _pool_matrix(1024, w).T
        blk = AT[:, qi * 256:(qi + 1) * 256]
        pA[g] = blk.reshape(8, 128, 256).transpose(1, 0, 2)
    return dict(c_cmask=cmask, c_ropeC=ropeC, c_ropeS=ropeS, c_wmask=wmask, c_poolAs=pA.reshape(4, 128, 2048))


def _fm(v):
    v = np.asarray(v, np.float32)
    lead = v.shape[:-1]
    m = v.shape[-1] // 128
    return np.ascontiguousarray(np.moveaxis(v.reshape(lead + (m, 128)), -1, 0))


_NC_CACHE = {}
_DEBUG = {}


def make_in_maps(x_prompt, x_sample, cache_a_k, cache_a_v, cache_c_k, cache_c_v, c, c_ctx,
                 w_mod, b_mod, norm_mix, norm_ffn, w_in_ab, rpb_a, w_pool, pool_scale, w_out_ab,
                 w_in_c, sink_c, w_out_c, w_gate_up, w_down, norm_final):
    f = lambda a: np.ascontiguousarray(np.asarray(a, dtype=np.float32))
    x_prompt, x_sample = f(x_prompt), f(x_sample)
    cache_a_k, cache_a_v, cache_c_k, cache_c_v = f(cache_a_k), f(cache_a_v), f(cache_c_k), f(cache_c_v)
    c, c_ctx = f(c), f(c_ctx)
    w_in_c = f(w_in_c)
    rpb_a = f(rpb_a)
    sink_c = f(sink_c)

    d = np.arange(64)
    partner = (d // 32) * 32 + (1 - (d % 32) // 16) * 16 + (d % 16)
    qcols = np.arange(1024)
    qperm = (qcols // 64) * 64 + partner[qcols % 64]
    kdup = np.concatenate([1024 + g * 64 + d for g in range(4) for _ in range(2)])
    kdup_perm = np.concatenate([1024 + g * 64 + partner for g in range(4) for _ in range(2)])
    kv = np.arange(1024, 1536)
    colidx = np.concatenate([qcols, kdup, qperm, kdup_perm, kv])
    w_in_cx = np.ascontiguousarray(w_in_c[:, :, colidx])

    cp = np.arange(64)[:, None]
    cq = np.arange(64)[None, :]
    dcol = np.clip(cp - cq + 15, 0, 30)
    nabs = []
    for qi in range(4):
        t_ = np.zeros((2, 8, 2, 64, 8, 4, 64), np.float32)
        for e in range(2):
            for m in range(8):
                for rl in range(4):
                    row = min(max(2 * m + e - (4 * qi + rl) + 7, 0), 14)
                    t_[:, :, e, :, m, rl, :] = rpb_a[:, :, row][:, :, dcol]
        nabs.append(t_.reshape(2, 8, 128, 2048))

    shared = dict(
        gmixT=_fm(norm_mix).reshape(128, -1),
        gffnT=_fm(norm_ffn).reshape(128, -1), gfinT=_fm(norm_final).reshape(128, -1),
        w_in_ab=f(w_in_ab), w_out_ab=f(w_out_ab), w_pool=f(w_pool), pscT=_fm(pool_scale).reshape(128, -1),
        w_in_cx=w_in_cx, w_out_c=f(w_out_c),
        sinkT=np.ascontiguousarray(np.stack([sink_c[:, 0::2], sink_c[:, 1::2]], 0).repeat(64, axis=0).reshape(128, 16)),
        w_gu=f(w_gate_up), w_dn=f(w_down), **_consts_shared())
    qconst = [_consts_quarter(qi) for qi in range(4)]
    w_mod = f(w_mod)
    b_mod = f(b_mod)
    wmq = [np.ascontiguousarray(w_mod[:, :, qi * 1536:(qi + 1) * 1536]) for qi in range(4)]
    bmq = [_fm(b_mod[:, qi * 1536:(qi + 1) * 1536]).reshape(128, 48) for qi in range(4)]
    in_maps = []
    for core in range(8):
        b = core // 4
        qi = core % 4
        xin = np.concatenate([x_prompt[2 * core].reshape(256, 1024), x_prompt[2 * core + 1].reshape(256, 1024),
                              x_sample[b, qi * 256:(qi + 1) * 256]], 0)
        cvec = np.stack([c_ctx, c[b]], 0)
        cvT = np.ascontiguousarray(_fm(cvec).transpose(0, 2, 1)).reshape(128, 16)
        cck = np.ascontiguousarray(cache_c_k[b].reshape(2, 512, 4, 1, 64).repeat(2, axis=3).reshape(2, 512, 512))
        m = dict(shared)
        m.update(qconst[qi])
        m.update(xin=np.ascontiguousarray(xin), cvT=cvT, nab=nabs[qi], w_mod=wmq[qi], bmodT=bmq[qi],
                 cak=np.ascontiguousarray(cache_a_k[b].reshape(2, 512, 512)),
                 cav=np.ascontiguousarray(cache_a_v[b].reshape(2, 512, 512)),
                 cck=cck, ccv=np.ascontiguousarray(cache_c_v[b].reshape(2, 512, 4, 1, 64).repeat(2, axis=3).reshape(2, 512, 512)))
        in_maps.append(m)
    return in_maps


def kernel(**inputs):
    if "nc" not in _NC_CACHE:
        _NC_CACHE["nc"] = build_program(*_DEBUG.get("args", ()))
    nc = _NC_CACHE["nc"]
    in_maps = make_in_maps(**inputs)
    res = run_bass_kernel_spmd(nc, in_maps, core_ids=list(range(8)))
    R = res.results
    y_prompt = np.concatenate([R[i]["o_yp"].reshape(2, 256, 1024) for i in range(8)], 0)
    y_sample = np.concatenate([R[i]["o_ys"] for i in range(8)], 0).reshape(2, 1024, 1024)
    nak = np.concatenate([R[i]["o_nak"] for i in range(8)], 0).reshape(16, 2, 256, 8, 64)
    nav = np.concatenate([R[i]["o_nav"] for i in range(8)], 0).reshape(16, 2, 256, 8, 64)
    nck = np.concatenate([R[i]["o_nck"] for i in range(8)], 0).reshape(16, 2, 256, 4, 64)
    ncv = np.concatenate([R[i]["o_ncv"] for i in range(8)], 0).reshape(16, 2, 256, 4, 64)
    return (y_prompt.astype(np.float32), y_sample.astype(np.float32), nak.astype(np.float32),
            nav.astype(np.float32), nck.astype(np.float32), ncv.astype(np.float32))
```

```python
from contextlib import ExitStack
import numpy as np
import concourse.bass as bass
import concourse.mybir as mybir
from concourse.bass_utils import run_bass_kernel_spmd

F32 = mybir.dt.float32
BF16 = mybir.dt.bfloat16
AF = mybir.ActivationFunctionType
ALU = mybir.AluOpType

ENGS = ("pe", "act", "dve", "pool", "sp")
DMA_CH = 6
NEG = -30000.0
NSLOT = 6
EPS = 1e-6
NT = 768
NP = 512
FFN = 2816


class _Op:
    __slots__ = ("eng", "fn", "reads", "writes", "dma", "deps", "idx", "sig", "chan", "chan_cnt")


class Sched:
    def __init__(self):
        self.ops = []
        self.last_w = {}
        self.readers = {}
        self.ndma = {e: 0 for e in ENGS}
        self.ncc = 0
        self.cc_inc = 1

    def add(self, eng, fn, reads=(), writes=(), dma=False):
        o = _Op()
        o.eng, o.fn, o.dma = eng, fn, dma
        o.reads, o.writes = tuple(reads), tuple(writes)
        o.idx = len(self.ops)
        o.sig = None
        o.chan = None
        deps = set()
        for r in o.reads:
            w = self.last_w.get(r)
            if w is not None:
                deps.add(w)
        for w_ in o.writes:
            w = self.last_w.get(w_)
            if w is not None:
                deps.add(w)
            for rd in self.readers.get(w_, ()):
                deps.add(rd)
        for r in o.reads:
            self.readers.setdefault(r, []).append(o.idx)
        for w_ in o.writes:
            self.last_w[w_] = o.idx
            self.readers[w_] = []
        keep = set()
        for d in deps:
            p = self.ops[d]
            if not p.dma and not o.dma and p.eng == o.eng:
                if o.eng == "pe":
                    continue
            keep.add(d)
        o.deps = keep
        if dma == "cc":
            self.ncc += 1
            o.chan = "cc"
            o.chan_cnt = self.cc_inc * self.ncc
        elif dma:
            n = self.ndma[eng]
            self.ndma[eng] = n + 1
            o.chan = n % DMA_CH
            o.chan_cnt = 16 * (n // DMA_CH + 1)
        self.ops.append(o)
        return o

    def emit(self, nc):
        ops = self.ops
        needed = set()
        for o in ops:
            needed |= o.deps
        cnt = {e: 0 for e in ENGS}
        for o in ops:
            if o.dma:
                continue
            if o.idx in needed:
                cnt[o.eng] += 1
                o.sig = cnt[o.eng]
        with ExitStack() as es:
            esem = {e: es.enter_context(nc.semaphore("s_" + e)) for e in ENGS}
            dsem = {}
            for e in ENGS:
                for c in range(min(DMA_CH, self.ndma[e])):
                    dsem[(e, c)] = es.enter_context(nc.semaphore("d_%s%d" % (e, c)))
            if self.ncc:
                dsem[("pool", "cc")] = es.enter_context(nc.semaphore("d_cc"))
            waits = {}
            waited = {e: {} for e in ENGS}
            for o in ops:
                wl = {}
                if o.dma == "cc":
                    if o.chan_cnt > self.cc_inc:
                        wl[("d", o.eng, o.chan)] = o.chan_cnt - self.cc_inc
                elif o.dma and o.chan_cnt > 16:
                    wl[("d", o.eng, o.chan)] = o.chan_cnt - 16
                for d in o.deps:
                    p = ops[d]
                    if p.dma:
                        key, val = ("d", p.eng, p.chan), p.chan_cnt
                    else:
                        key, val = ("e", p.eng), p.sig
                    if wl.get(key, 0) < val:
                        wl[key] = val
                out = []
                for key, val in wl.items():
                    if waited[o.eng].get(key, 0) >= val:
                        continue
                    waited[o.eng][key] = val
                    out.append((key, val))
                waits[o.idx] = out

            def run(engname, eng):
                for o in ops:
                    if o.eng != engname:
                        continue
                    for key, val in waits[o.idx]:
                        s = esem[key[1]] if key[0] == "e" else dsem[(key[1], key[2])]
                        eng.wait_ge(s, val)
                    ins = o.fn(eng)
                    if o.dma == "cc":
                        ins.then_inc(dsem[(o.eng, o.chan)], self.cc_inc)
                    elif o.dma:
                        ins.then_inc(dsem[(o.eng, o.chan)], 16)
                    elif o.sig is not None:
                        ins.then_inc(esem[o.eng], 1)
                if engname == "pool" and self.ncc:
                    eng.wait_ge(dsem[("pool", "cc")], self.cc_inc * self.ncc)
                n = self.ndma[engname]
                for c in range(min(DMA_CH, n)):
                    k = (n - 1 - c) // DMA_CH + 1
                    eng.wait_ge(dsem[(engname, c)], 16 * k)

            with nc.Block() as block:
                @block.tensor
                def _(eng):
                    run("pe", eng)

                @block.scalar
                def _(eng):
                    run("act", eng)

                @block.vector
                def _(eng):
                    run("dve", eng)

                @block.gpsimd
                def _(eng):
                    run("pool", eng)

                @block.sync
                def _(eng):
                    run("sp", eng)


def build_program(depth=4, stop=None):
    nc = bass.Bass("TRN2", target_bir_lowering=False)

    def din(name, shape):
        return nc.dram_tensor(name, list(shape), F32, kind="ExternalInput").ap()

    def dout(name, shape):
        return nc.dram_tensor(name, list(shape), F32, kind="ExternalOutput").ap()

    xin = din("xin", [NT, 1024])
    cvT = din("cvT", [128, 16])
    w_mod = din("w_mod", [4, 1024, 1536])
    bmodT = din("bmodT", [128, 48])
    gmixT = din("gmixT", [128, 32])
    gffnT = din("gffnT", [128, 32])
    gfinT = din("gfinT", [128, 8])
    w_in_ab = din("w_in_ab", [2, 1024, 2048])
    w_out_ab = din("w_out_ab", [2, 1024, 1024])
    w_pool = din("w_pool", [2, 4, 128, 128])
    pscT = din("pscT", [128, 8])
    w_in_cx = din("w_in_cx", [2, 1024, 3584])
    w_out_c = din("w_out_c", [2, 1024, 1024])
    sinkT = din("sinkT", [128, 16])
    w_gu = din("w_gu", [4, 1024, 2 * FFN])
    w_dn = din("w_dn", [4, FFN, 1024])
    cak = din("cak", [2, 512, 512])
    cav = din("cav", [2, 512, 512])
    cck = din("cck", [2, 512, 512])
    ccv = din("ccv", [2, 512, 512])
    nab = din("nab", [2, 8, 128, 2048])
    c_ident = din("c_ident", [128, 128])
    c_cmask = din("c_cmask", [128, 2048])
    c_ropeC = din("c_ropeC", [128, 256])
    c_ropeS = din("c_ropeS", [128, 256])
    c_wmask = din("c_wmask", [128, 2048])
    c_poolA = din("c_poolA", [128, 4 * 5 * 128])
    c_poolAs = din("c_poolAs", [4, 128, 2048])

    o_yp = dout("o_yp", [512, 1024])
    o_ys = dout("o_ys", [256, 1024])
    o_nak = dout("o_nak", [2, 2, 256, 512])
    o_nav = dout("o_nav", [2, 2, 256, 512])
    o_nck = dout("o_nck", [2, 2, 256, 256])
    o_ncv = dout("o_ncv", [2, 2, 256, 256])

    SLW = [3072, 2048, 3072, 2048]
    slab = [nc.dram_tensor("slab%d" % l, [128, SLW[l]], BF16) for l in range(4)]
    gath = [nc.dram_tensor("gath%d" % l, [512, SLW[l]], BF16) for l in range(4)]
    RG = [[0, 1, 2, 3], [4, 5, 6, 7]]
    mslab0 = nc.dram_tensor("mslab0", [128, 24], F32)
    mgath0 = nc.dram_tensor("mgath0", [512, 24], F32)
    mslab1 = nc.dram_tensor("mslab1", [128, 72], F32)
    mgath1 = nc.dram_tensor("mgath1", [512, 72], F32)

    CH = [(0, 512), (512, 256)]

    S = Sched()
    with ExitStack() as es:
        def sb(name, shape, dt):
            return es.enter_context(nc.sbuf_tensor(name, list(shape), dt))

        xT = sb("xT", [128, 8, NT], F32)
        HM = sb("HM", [128, 8, NT], BF16)
        U1 = sb("U1", [128, 22 * NT], BF16)
        qkT = U1[:, 0:22 * NT].rearrange("p (a t) -> p a t", a=22)
        TM = sb("TM", [128, 4, 512], BF16)
        ring = [sb("ring%d" % i, [128, 2048], BF16) for i in range(NSLOT)]
        KsT = sb("KsT", [128, 4, 1024], BF16)
        Vs = sb("Vs", [128, 8, 512], BF16)
        Us = sb("Us", [128, 8, 512], BF16)
        SL = sb("SL", [128, 3072], BF16)
        Eb = [sb("E%d" % i, [128, 1024], BF16) for i in range(2)]
        T2 = sb("T2", [128, 2, 2048], BF16)
        T2s = sb("T2s", [128, 2048], BF16)
        qz = [sb("qz%d" % i_, [128, 2, 256], BF16) for i_ in range(2)]
        cmask = sb("cmask", [128, 2048], BF16)
        wmask = sb("wmask", [128, 8, 256], BF16)
        poolAs = sb("poolAs", [128, 8, 256], BF16)
        ctxKT = sb("ctxKT", [128, 4, 512], BF16)
        ctxV = sb("ctxV", [128, 4, 512], BF16)
        ropeC = sb("ropeC", [128, 256], F32)
        ropeS = sb("ropeS", [128, 256], F32)
        stg = [sb("stg%d" % i, [128, 1024], F32) for i in range(3)]
        sq = [sb("sq%d" % i, [128, 512], BF16) for i in range(2)]
        tmp = [sb("tmp%d" % i, [128, 512], F32) for i in range(3)]
        rt = sb("rt", [128, 512], F32)
        rstd = sb("rstd", [128, 512], F32)
        rden = sb("rden", [128, 512], F32)
        pooled = [sb("pooled%d" % i, [128, 512], BF16) for i in range(2)]
        ident = sb("ident", [128, 128], F32)
        identb = sb("identb", [128, 128], BF16)
        ones = sb("ones", [128, 128], BF16)
        poolA = sb("poolA", [128, 4, 5, 128], BF16)
        wpool = sb("wpool", [128, 4, 128], BF16)
        modT = sb("modT", [128, 4, 48, 2], F32)
        modP = sb("modP", [128, 48, 2], F32)
        Aab = sb("Aab", [128, 4, 2, 8, 2], F32)
        bmod = sb("bmod", [128, 48], F32)
        gmix = sb("gmix", [128, 4, 8], F32)
        gffn = sb("gffn", [128, 4, 8], F32)
        gfin = sb("gfin", [128, 8], F32)
        psc = sb("psc", [128, 2, 4], F32)
        sinkE = sb("sinkE", [128, 2, 8], F32)
        cv = sb("cv", [128, 8, 2], F32)
        cvb = sb("cvb", [128, 8, 2], BF16)
        ps = [es.enter_context(nc.psum_tensor("ps%d" % i, [128, 512], F32)) for i in range(8)]

        st = {"ring": 0, "bank": 0, "alt": 0, "tmp": 0, "sq": 0, "stg": 0, "reserved": set()}

        def PS(b):
            return ("ps", b)

        def nextbank():
            while True:
                b = st["bank"]
                st["bank"] = (b + 1) % 8
                if b not in st["reserved"]:
                    return b

        def evac_eng():
            st["alt"] ^= 1
            return "act" if st["alt"] else "dve"

        def ACT(out, in_, func, reads, writes, bias=None, scale=None):
            kw = {}
            if bias is not None:
                kw["bias"] = bias
            if scale is not None:
                kw["scale"] = scale
            S.add("act", lambda e: e.activation(out, in_, func, **kw), reads=reads, writes=writes)

        def TT(out, a, b_, op, reads, writes):
            S.add("dve", lambda e: e.tensor_tensor(out, a, b_, op), reads=reads, writes=writes)

        def TS(out, a, s1, op0, reads, writes):
            S.add("dve", lambda e: e.tensor_scalar(out, a, s1, None, op0), reads=reads, writes=writes)

        def STT(out, in0, scalar, in1, op0, op1, reads, writes):
            S.add("dve", lambda e: e.scalar_tensor_tensor(out, in0, scalar, in1, op0, op1), reads=reads, writes=writes)

        def RECIP(out, in_, reads, writes):
            S.add("dve", lambda e: e.reciprocal(out, in_), reads=reads, writes=writes)

        def TR(out, in_, reads, writes):
            S.add("pe", lambda e: e.transpose(out, in_, ident[:]), reads=list(reads) + ["ident"], writes=writes)

        def DMA(q, out, in_, reads=(), writes=()):
            S.add(q, lambda e: e.dma_start(out=out, in_=in_), reads=reads, writes=writes, dma=True)

        def copy_op(eng, out, in_, reads, writes):
            if eng == "act":
                ACT(out, in_, AF.Identity, reads, writes)
            else:
                S.add("dve", lambda e: e.tensor_copy(out, in_), reads=reads, writes=writes)

        def mm(out, lhsT, rhs, start, stop, reads, pskey):
            rr = []
            for r_ in reads:
                if hasattr(r_, "keys") and not isinstance(r_, (tuple, str)):
                    rr.extend(r_.keys)
                else:
                    rr.append(r_)
            S.add("pe", lambda e: e.matmul(out, lhsT, rhs, start=start, stop=stop), reads=rr, writes=[pskey])

        class Piece:
            def __init__(self, views, keys):
                self.views, self.keys = views, keys

            def __getitem__(self, idx):
                p, k, cs = idx
                a_, b_ = cs.start, cs.stop
                h = a_ // 256
                assert (b_ - 1) // 256 == h
                return self.views[h][:, k, a_ - h * 256:b_ - h * 256]

        class PieceKeys:
            def __init__(self, keys):
                self.keys = keys

        def piece(src2d, KT, NC):
            assert KT <= 8
            views, keys = [], []
            for h in range((NC + 255) // 256):
                ncol = min(256, NC - h * 256)
                s_ = st["ring"]
                st["ring"] = (s_ + 1) % NSLOT
                view = ring[s_][:, 0:KT * ncol].rearrange("p (k n) -> p k n", k=KT)
                DMA("pool", view, src2d[:, h * 256:h * 256 + ncol].rearrange("(k p) n -> p k n", p=128), writes=[("w", s_)])
                views.append(view)
                keys.append(("w", s_))
            return Piece(views, keys), PieceKeys(keys)

        def load_f32(dst, src, key):
            DMA("sp", dst, src, writes=[key])

        def load_cast(dst, src, key):
            DMA("pool", dst, src, writes=[key])

        def next_tmp():
            ti = st["tmp"]
            st["tmp"] = (ti + 1) % 3
            return ti

        def next_stg():
            si = st["stg"]
            st["stg"] = (si + 1) % 3
            return si

        def csl(c):
            return slice(CH[c][0], CH[c][0] + CH[c][1])

        def cw(c):
            return CH[c][1]

        load_f32(cv[:].rearrange("p k j -> p (k j)"), cvT, "cv")
        load_f32(bmod[:], bmodT, "bmod")
        load_f32(ident[:], c_ident, "ident")
        load_f32(gmix[:].rearrange("p l k -> p (l k)"), gmixT, "gmix")
        load_f32(gffn[:].rearrange("p l k -> p (l k)"), gffnT, "gffn")
        load_f32(gfin[:], gfinT, "gfin")
        load_f32(psc[:].rearrange("p l g -> p (l g)"), pscT, "psc")
        load_f32(sinkE[:].rearrange("p l g -> p (l g)"), sinkT, "sinkE")
        load_f32(ropeC[:], c_ropeC, "ropeC")
        load_f32(ropeS[:], c_ropeS, "ropeS")
        S.add("dve", lambda e: e.memset(ones[:], 1.0), writes=["ones"])
        for i_ in range(2):
            S.add("dve", lambda e, i_=i_: e.memset(qz[i_][:], 0.0), writes=[("qz", i_, 0), ("qz", i_, 1)])
        ACT(cvb[:].rearrange("p k j -> p (k j)"), cv[:].rearrange("p k j -> p (k j)"), AF.Silu, ["cv"], ["cvb"])
        ACT(sinkE[:].rearrange("p l g -> p (l g)"), sinkE[:].rearrange("p l g -> p (l g)"), AF.Exp, ["sinkE"], ["sinkE"])
        TS(gmix[:].rearrange("p l k -> p (l k)"), gmix[:].rearrange("p l k -> p (l k)"), 32.0, ALU.mult, ["gmix"], ["gmix"])
        TS(gffn[:].rearrange("p l k -> p (l k)"), gffn[:].rearrange("p l k -> p (l k)"), 32.0, ALU.mult, ["gffn"], ["gffn"])
        TS(gfin[:], gfin[:], 32.0, ALU.mult, ["gfin"], ["gfin"])

        def late_consts():
            load_cast(identb[:], c_ident, "identb")
            load_cast(cmask[:], c_cmask, "cmask")
            load_cast(wmask[:].rearrange("p a k -> p (a k)"), c_wmask, "wmask")
            load_cast(poolA[:].rearrange("p g a t -> p (g a t)"), c_poolA, "poolA")

        def aab_layer(l):
            for which, g_, off, nm in ((0, gmix, 8, "gmix"), (1, gffn, 32, "gffn")):
                for j in range(2):
                    STT(Aab[:, l, which, :, j], modT[:, l, off:off + 8, j], 1.0, g_[:, l, :], ALU.add, ALU.mult,
                        [("modT", l), nm], [("Aab", l)])

        def mod_prologue():
            b = nextbank()
            for pc in range(3):
                wv, wk = piece(w_mod[0, :, pc * 512:(pc + 1) * 512], 8, 512)
                for mm_ in range(4):
                    m = pc * 4 + mm_
                    for k in range(8):
                        mm(ps[b][:, m * 2:(m + 1) * 2], wv[:, k, mm_ * 128:(mm_ + 1) * 128], cvb[:, k, :],
                           k == 0, k == 7, [wk, "cvb"], PS(b))
            for j in range(2):
                TT(modP[:, 0:12, j], ps[b][:, 0:24].rearrange("p (m j) -> p m j", j=2)[:, :, j], bmod[:, 0:12], ALU.add,
                   [PS(b), "bmod"], ["modP0"])
            DMA("sp", mslab0.ap(), modP[:, 0:12, :].rearrange("p m j -> p (m j)"), reads=["modP0"], writes=["mslab0"])
            S.add("pool", lambda e: e.collective_compute("AllGather", ALU.bypass, replica_groups=RG,
                                                         ins=[mslab0.ap().opt()], outs=[mgath0.ap().opt()]),
                  reads=["mslab0"], writes=["mgath0"], dma="cc")

        def mod_prologue_recv():
            for r in range(4):
                DMA("sp", modT[:, 0, r * 12:(r + 1) * 12, :],
                    mgath0.ap()[r * 128:(r + 1) * 128, :].rearrange("p (m j) -> p m j", m=12),
                    reads=["mgath0"], writes=[("modT", 0)])
            aab_layer(0)

        mod_todo = [(l, h) for l in range(1, 4) for h in range(6)]

        def mod_step():
            if not mod_todo or depth < 2:
                return
            l, h = mod_todo.pop(0)
            wv, wk = piece(w_mod[l, :, h * 256:(h + 1) * 256], 8, 256)
            b = nextbank()
            for mm_ in range(2):
                for k in range(8):
                    mm(ps[b][:, mm_ * 2:(mm_ + 1) * 2], wv[:, k, mm_ * 128:(mm_ + 1) * 128], cvb[:, k, :],
                       k == 0, k == 7, [wk, "cvb"], PS(b))
            i0 = l * 12 + h * 2
            for j in range(2):
                TT(modP[:, i0:i0 + 2, j], ps[b][:, 0:4].rearrange("p (m j) -> p m j", j=2)[:, :, j], bmod[:, i0:i0 + 2], ALU.add,
                   [PS(b), "bmod"], ["modP1"])

        mod_state = {"finished": False}

        def mod_finish():
            if depth < 2 or mod_state["finished"]:
                return
            mod_state["finished"] = True
            while mod_todo:
                mod_step()
            DMA("sp", mslab1.ap(), modP[:, 12:48, :].rearrange("p m j -> p (m j)"), reads=["modP1"], writes=["mslab1"])
            S.add("pool", lambda e: e.collective_compute("AllGather", ALU.bypass, replica_groups=RG,
                                                         ins=[mslab1.ap().opt()], outs=[mgath1.ap().opt()]),
                  reads=["mslab1"], writes=["mgath1"], dma="cc")

        def mod_finish_recv():
            if depth < 2:
                return
            for r in range(4):
                DMA("sp", modT[:, 1:4, r * 12:(r + 1) * 12, :],
                    mgath1.ap()[r * 128:(r + 1) * 128, :].rearrange("p (l m j) -> p l m j", l=3, m=12),
                    reads=["mgath1"], writes=[("modT", 1), ("modT", 2), ("modT", 3)])
            for l in range(1, 4):
                aab_layer(l)

        SSB = {0: 6, 1: 7}

        def ss_accum(k, c):
            w = cw(c)
            si = st["sq"]
            st["sq"] ^= 1
            ACT(sq[si][:, 0:w], xT[:, k, csl(c)], AF.Square, [("x", k, c)], [("sq", si)])
            mm(ps[SSB[c]][:, 0:w], ones[:], sq[si][:, 0:w], k == 0, k == 7, ["ones", ("sq", si)], PS(SSB[c]))

        ss_pending = []

        def ss_defer(k, c):
            ss_pending.append((k, c))
            if len(ss_pending) > 6:
                ss_accum(*ss_pending.pop(0))

        def ss_flush():
            while ss_pending:
                ss_accum(*ss_pending.pop(0))

        def rstd_finish(c):
            w = cw(c)
            ACT(rt[:, 0:w], ps[SSB[c]][:, 0:w], AF.Ln, [PS(SSB[c])], ["rt"], bias=1024 * EPS, scale=1.0)
            ACT(rstd[:, 0:w], rt[:, 0:w], AF.Exp, ["rt"], ["rstd"], scale=-0.5)

        def rstd_chunk(c):
            for k in range(8):
                ss_accum(k, c)
            rstd_finish(c)

        def norm_phase(l, which, off_b, pre=False, order=(0, 1)):
            for c in order:
                j = c
                w = cw(c)
                if pre:
                    rstd_finish(c)
                else:
                    rstd_chunk(c)
                for k in range(8):
                    ti = next_tmp()
                    TT(tmp[ti][:, 0:w], xT[:, k, csl(c)], rstd[:, 0:w], ALU.mult, [("x", k, c), "rstd"], [("tmp", ti)])
                    ACT(HM[:, k, csl(c)], tmp[ti][:, 0:w], AF.Identity, [("tmp", ti), ("modT", l), ("Aab", l)], [("hm", k, c)],
                        bias=modT[:, l, off_b + k, j:j + 1], scale=Aab[:, l, which, k, j:j + 1])

        def std_group(wv, wk, m, c):
            b = nextbank()
            w = cw(c)
            for k in range(8):
                mm(ps[b][:, 0:w], wv[:, k, m * 128:(m + 1) * 128], HM[:, k, csl(c)],
                   k == 0, k == 7, [wk, ("hm", k, c)], PS(b))
            return b

        def tok_proj(wv, wk, ncols, blocks, on_block):
            for blk in blocks:
                c = 0 if blk < 4 else 1
                b = nextbank()
                for h in range(ncols // 256):
                    for k in range(8):
                        mm(ps[b][:, h * 256:(h + 1) * 256], HM[:, k, blk * 128:(blk + 1) * 128], wv[:, k, h * 256:(h + 1) * 256],
                           k == 0, k == 7, [wk, ("hm", k, c)], PS(b))
                on_block(blk, b)

        def out_stage(b, ncols, col0, dst_dram, eng=None):
            si = next_stg()
            copy_op(eng or evac_eng(), stg[si][:, 0:ncols], ps[b][:, col0:col0 + ncols], [PS(b)], [("stg", si)])
            DMA("sp", dst_dram, stg[si][:, 0:ncols], reads=[("stg", si)])

        att = {"S": 0, "OD": 0, "qz": 0}

        def attend_A(qz_ap, q_reads, N, kblocks):
            sbuf_i = att["S"]
            att["S"] ^= 1
            banks = (0, 1) if sbuf_i == 0 else (2, 3)
            nkb = len(kblocks)
            assert nkb * N <= 1024
            per_bank = 512 // N
            for i, kb in enumerate(kblocks):
                bk = banks[i // per_bank]
                col = (i % per_bank) * N
                has_bias = kb[4] is not None
                mm(ps[bk][:, col:col + N], kb[0], qz_ap, True, not has_bias, list(kb[1]) + list(q_reads), PS(bk))
                if has_bias:
                    mm(ps[bk][:, col:col + N], identb[:], kb[4], False, True, ["identb"] + list(kb[5]), PS(bk))
            E = Eb[sbuf_i]
            nb_used = (nkb + per_bank - 1) // per_bank
            for bi in range(nb_used):
                ncol = min(nkb - bi * per_bank, per_bank) * N
                bk = banks[bi]
                ACT(E[:, bi * 512:bi * 512 + ncol], ps[bk][:, 0:ncol], AF.Exp, [PS(bk)], [("E", sbuf_i, bi)], scale=0.125)
            return sbuf_i

        def attend_B(sbuf_i, N, kblocks, obank, dbank, ocol, acc_start, acc_stop):
            E = Eb[sbuf_i]
            nkb = len(kblocks)
            per_bank = 512 // N
            for i, kb in enumerate(kblocks):
                bi = i // per_bank
                col = bi * 512 + (i % per_bank) * N
                mm(ps[obank][:, ocol:ocol + N], kb[2], E[:, col:col + N], acc_start and i == 0, acc_stop and i == nkb - 1,
                   list(kb[3]) + [("E", sbuf_i, bi)], PS(obank))
            for i, kb in enumerate(kblocks):
                bi = i // per_bank
                col = bi * 512 + (i % per_bank) * N
                mm(ps[dbank][:, ocol:ocol + N], ones[:], E[:, col:col + N], acc_start and i == 0, acc_stop and i == nkb - 1,
                   ["ones", ("E", sbuf_i, bi)], PS(dbank))

        def make_qz(q_tile, c0, q_key):
            bi = att["qz"]
            att["qz"] ^= 1
            for e_ in range(2):
                rows = slice(e_ * 64, (e_ + 1) * 64)
                S.add("dve", lambda e, e_=e_, rows=rows: e.tensor_copy(qz[bi][rows, e_, :], qkT[rows, q_tile, c0:c0 + 256]),
                      reads=[q_key], writes=[("qz", bi, e_)])
            return bi

        def od_banks():
            i = att["OD"]
            att["OD"] ^= 1
            return (4, 5) if i == 0 else (6, 7)

        def normalize(obank, dbank, dst_tile, c0, dst_key, sink_ap=None):
            if sink_ap is not None:
                ACT(rden[:], ps[dbank][:], AF.Ln, [PS(dbank), "sinkE"], ["rden"], bias=sink_ap, scale=1.0)
            else:
                ACT(rden[:], ps[dbank][:], AF.Ln, [PS(dbank)], ["rden"])
            ACT(rden[:], rden[:], AF.Exp, ["rden"], ["rden"], scale=-1.0)
            for e_ in range(2):
                rows = slice(e_ * 64, (e_ + 1) * 64)
                cols = slice(e_ * 256, (e_ + 1) * 256)
                TT(HM[rows, dst_tile, c0:c0 + 256], ps[obank][rows, cols], rden[rows, cols], ALU.mult,
                   [PS(obank), "rden"], [dst_key])

        def load_ctx(kd, vd, vcols):
            load_cast(ctxV[:, :, 0:vcols], vd.rearrange("(i p) f -> p i f", p=128), "ctxV")
            for blk in range(4):
                si = next_stg()
                load_f32(stg[si][:, 0:512], kd[blk * 128:(blk + 1) * 128, :], ("stg", si))
                b = nextbank()
                for j in range(4):
                    TR(ps[b][:, j * 128:(j + 1) * 128], stg[si][:, j * 128:(j + 1) * 128], [("stg", si)], [PS(b)])
                copy_op(evac_eng(), ctxKT[:, :, blk * 128:(blk + 1) * 128], ps[b][:].rearrange("p (j t) -> p j t", j=4),
                        [PS(b)], ["ctxKT"])

        def exchange(l, even):
            W = SLW[l]
            DMA("sp", slab[l].ap(), SL[:, 0:W], reads=["SLk", "SLv", "SLu"], writes=[("slab", l)])
            S.add("pool", lambda e: e.collective_compute("AllGather", ALU.bypass, replica_groups=RG,
                                                         ins=[slab[l].ap().opt()], outs=[gath[l].ap().opt()]),
                  reads=[("slab", l)], writes=[("gath", l)], dma="cc")

        def exchange_recv(l, even):
            g = gath[l].ap()
            vw = 512
            for r in range(4):
                rows = slice(r * 128, (r + 1) * 128)
                DMA("sp", KsT[:, :, r * 256:(r + 1) * 256], g[rows, 0:1024].rearrange("p (j t) -> p j t", j=4),
                    reads=[("gath", l)], writes=[("KsT", r)])
                DMA("sp", Vs[:, 2 * r:2 * r + 2, 0:vw], g[rows, 1024:1024 + 2 * vw].rearrange("p (b f) -> p b f", b=2),
                    reads=[("gath", l)], writes=[("Vs", r)])
                if even:
                    DMA("sp", Us[:, 2 * r:2 * r + 2, :], g[rows, 2048:3072].rearrange("p (b f) -> p b f", b=2),
                        reads=[("gath", l)], writes=[("Us", r)])

        def run_attention_units(units):
            nu = len(units)
            obs = [od_banks() for _ in units]
            qzb = [None] * nu
            calls = []
            for ui, u in enumerate(units):
                for e_ in range(2):
                    kbl = u["kbl_fn"](e_)
                    np_ = u["parts"]
                    per = len(kbl) // np_
                    for part in range(np_):
                        calls.append(dict(ui=ui, e=e_, kbl=kbl[part * per:(part + 1) * per], start=(part == 0), stop=(part == np_ - 1),
                                          last_of_head=(part == np_ - 1), last_of_unit=(e_ == 1 and part == np_ - 1)))

            def ensure_unit_ready(ui):
                if ui < nu and qzb[ui] is None:
                    u = units[ui]
                    qzb[ui] = make_qz(u["q_tile"], u["c0"], u["q_key"])

            def prep_head(ui, e_):
                if ui < nu and units[ui].get("prep") is not None:
                    units[ui]["prep"](e_)
            ensure_unit_ready(0)
            prep_head(0, 0)
            prep_head(0, 1)
            pending = None
            for c_ in calls:
                ui = c_["ui"]
                u = units[ui]
                ensure_unit_ready(ui)
                si = attend_A(qz[qzb[ui]][:, c_["e"], :], [("qz", qzb[ui], c_["e"])], 256, c_["kbl"])
                if c_["last_of_head"]:
                    prep_head(ui + 1, c_["e"])
                if c_["e"] == 0 and c_["start"]:
                    ensure_unit_ready(ui + 1)
                if pending is not None:
                    pc_, psi = pending
                    pob, pdb = obs[pc_["ui"]]
                    attend_B(psi, 256, pc_["kbl"], pob, pdb, pc_["e"] * 256, pc_["start"], pc_["stop"])
                    if pc_["last_of_unit"]:
                        pu = units[pc_["ui"]]
                        normalize(pob, pdb, pu["dst_tile"], pu["c0"], pu["dst_key"], sink_ap=pu.get("sink"))
                pending = (c_, si)
            pc_, psi = pending
            pob, pdb = obs[pc_["ui"]]
            attend_B(psi, 256, pc_["kbl"], pob, pdb, pc_["e"] * 256, pc_["start"], pc_["stop"])
            pu = units[pc_["ui"]]
            normalize(pob, pdb, pu["dst_tile"], pu["c0"], pu["dst_key"], sink_ap=pu.get("sink"))

        def sample_units(nheadpairs, kt_tile_fn, v_col_fn, ctx_tile_fn, ctxv_col_fn, table_fn, table_keys_fn,
                         sink_fn=None, prep_fn=None):
            units = []
            for jp in range(nheadpairs):
                def kbl_fn(e_, jp=jp):
                    kbl = []
                    for ib in range(4):
                        kbl.append((ctxKT[:, ctx_tile_fn(jp), ib * 128:(ib + 1) * 128], ["ctxKT"],
                                    ctxV[:, ib, ctxv_col_fn(jp)], ["ctxV"], None, None))
                    for m in range(8):
                        kbl.append((KsT[:, kt_tile_fn(jp), m * 128:(m + 1) * 128], [("KsT", m // 2)],
                                    Vs[:, m, v_col_fn(jp)], [("Vs", m // 2)],
                                    table_fn(jp, e_, m), table_keys_fn(jp, e_)))
                    return kbl
                units.append(dict(q_tile=jp, c0=CH[1][0], q_key=("qk", jp, 1), kbl_fn=kbl_fn, parts=3,
                                  dst_tile=jp, dst_key=("hm", jp, 1),
                                  sink=None if sink_fn is None else sink_fn(jp),
                                  prep=None if prep_fn is None else (lambda e_, jp=jp: prep_fn(jp, e_))))
            return units

        def prompt_units(nheadpairs, kt_tile_fn, v_col_fn, sink_fn=None):
            units = []
            for jp in range(nheadpairs):
                for s_ in range(2):
                    def kbl_fn(e_, jp=jp, s_=s_):
                        kbl = []
                        for kb in range(2):
                            blk = s_ * 2 + kb
                            kbl.append((qkT[:, kt_tile_fn(jp), blk * 128:(blk + 1) * 128], [("qk", kt_tile_fn(jp), 0)],
                                        TM[:, blk, v_col_fn(jp)], [("tm", blk)], None, None))
                        return kbl
                    units.append(dict(q_tile=jp, c0=s_ * 256, q_key=("qk", jp, 0), kbl_fn=kbl_fn, parts=1,
                                      dst_tile=jp, dst_key=("hm", jp, 0),
                                      sink=None if sink_fn is None else sink_fn(jp)))
            return units

        def resid_update(bank, m, c, gate_ap, l):
            w = cw(c)
            STT(xT[:, m, csl(c)], ps[bank][:, 0:w], gate_ap, xT[:, m, csl(c)], ALU.mult, ALU.add,
                [PS(bank), ("x", m, c), ("modT", l)], [("x", m, c)])

        def out_proj(l, wsrc, gate_off, src_fn):
            st["reserved"] = {6, 7}
            for pc in range(2):
                wv, wk = piece(wsrc[:, pc * 512:(pc + 1) * 512], 8, 512)
                for c in range(2):
                    w = cw(c)
                    for mm_ in range(4):
                        m = pc * 4 + mm_
                        b = nextbank()
                        for k in range(8):
                            ap_, key_ = src_fn(k, c)
                            mm(ps[b][:, 0:w], wv[:, k, mm_ * 128:(mm_ + 1) * 128], ap_, k == 0, k == 7, [wk, key_], PS(b))
                        resid_update(b, m, c, modT[:, l, gate_off + m, c:c + 1], l)
                        ss_defer(m, c)
            ss_flush()

        def ffn_phase(l):
            hid = qkT
            for t0 in range(0, 22, 2):
                gcol = t0 * 128
                wg, wgk = piece(w_gu[l, :, gcol:gcol + 256], 8, 256)
                wu, wuk = piece(w_gu[l, :, FFN + gcol:FFN + gcol + 256], 8, 256)
                for c in range(2):
                    w = cw(c)
                    for t in range(2):
                        bg = std_group(wg, wgk, t, c)
                        bu = std_group(wu, wuk, t, c)
                        ti = next_tmp()
                        ACT(tmp[ti][:, 0:w], ps[bg][:, 0:w], AF.Silu, [PS(bg)], [("tmp", ti)])
                        TT(hid[:, t0 + t, csl(c)], ps[bu][:, 0:w], tmp[ti][:, 0:w], ALU.mult,
                           [PS(bu), ("tmp", ti)], [("qk", t0 + t, c)])
                if l == 0:
                    mod_step()
                    if t0 < 14:
                        mod_step()
            if l == 0:
                mod_finish()
            st["reserved"] = {6, 7}
            kb_ = [0, 8, 16, 22]
            for pc in range(4):
                pcs = [piece(w_dn[l, kb_[j_] * 128:kb_[j_ + 1] * 128, pc * 256:(pc + 1) * 256], kb_[j_ + 1] - kb_[j_], 256)
                       for j_ in range(3)]
                for c in range(2):
                    w = cw(c)
                    for mm_ in range(2):
                        m = pc * 2 + mm_
                        b = nextbank()
                        for k in range(22):
                            j_ = max(jj for jj in range(3) if kb_[jj] <= k)
                            wv_, wk_ = pcs[j_]
                            mm(ps[b][:, 0:w], wv_[:, k - kb_[j_], mm_ * 128:(mm_ + 1) * 128], hid[:, k, csl(c)],
                               k == 0, k == 21, [wk_, ("qk", k, c)], PS(b))
                        resid_update(b, m, c, modT[:, l, 40 + m, c:c + 1], l)
                        ss_defer(m, c)
            ss_flush()
            if l == 0:
                mod_finish()
                mod_finish_recv()

        def mixer_ab(l, i):
            wv_k, wk_k = piece(w_in_ab[i, :, 512:1024], 8, 512)
            for m in range(4):
                b = std_group(wv_k, wk_k, m, 1)
                copy_op(evac_eng(), SL[:, m * 256:(m + 1) * 256], ps[b][:, 0:256], [PS(b)], ["SLk"])
            wv_v, wk_v = piece(w_in_ab[i, :, 1024:1536], 8, 512)
            tok_proj(wv_v, wk_v, 512, (4, 5),
                     lambda blk, b: copy_op(evac_eng(), SL[:, 1024 + (blk - 4) * 512:1024 + (blk - 3) * 512], ps[b][:], [PS(b)], ["SLv"]))
            wv_u, wk_u = piece(w_in_ab[i, :, 1536:2048], 8, 512)
            tok_proj(wv_u, wk_u, 512, (4, 5),
                     lambda blk, b: copy_op(evac_eng(), SL[:, 2048 + (blk - 4) * 512:2048 + (blk - 3) * 512], ps[b][:], [PS(b)], ["SLu"]))
            exchange(l, True)
            for m in range(4):
                b = std_group(wv_k, wk_k, m, 0)
                copy_op(evac_eng(), qkT[:, 4 + m, csl(0)], ps[b][:], [PS(b)], [("qk", 4 + m, 0)])
            tok_proj(wv_k, wk_k, 512, range(4),
                     lambda blk, b: out_stage(b, 512, 0, o_nak[blk // 2, i, (blk % 2) * 128:(blk % 2 + 1) * 128, :]))
            wv_q, wk_q = piece(w_in_ab[i, :, 0:512], 8, 512)
            tok_proj(wv_u, wk_u, 512, range(4),
                     lambda blk, b: copy_op(evac_eng(), TM[:, blk, :], ps[b][:], [PS(b)], [("tm", blk)]))

            def mixb_tail(b, g, c, pi):
                w = cw(c)
                copy_op("act", pooled[pi][:, 0:w], ps[b][:, 0:w], [PS(b)], [("pooled", pi)])
                b2 = nextbank()
                mm(ps[b2][:, 0:w], wpool[:, g, :], pooled[pi][:, 0:w], True, True, ["wpool", ("pooled", pi)], PS(b2))
                ACT(qkT[:, 8 + g, csl(c)], ps[b2][:, 0:w], AF.Identity, [PS(b2), "psc"], [("qk", 8 + g, c)],
                    scale=psc[:, i, g:g + 1])
            for g in range(4):
                b = nextbank()
                for tb in range(4):
                    blk = tb
                    first, last = (blk % 2 == 0), (blk % 2 == 1)
                    srcs = []
                    if not first:
                        srcs.append((blk - 1, 3))
                    srcs.append((blk, 0 if first else 2))
                    if not last:
                        srcs.append((blk + 1, 4))
                    for n_, (sblk, typ) in enumerate(srcs):
                        mm(ps[b][:, tb * 128:(tb + 1) * 128], TM[:, sblk, g * 128:(g + 1) * 128], poolA[:, g, typ, :],
                           n_ == 0, n_ == len(srcs) - 1, [("tm", sblk), "poolA"], PS(b))
                mixb_tail(b, g, 0, g % 2)

            def v_block(blk, b):
                eng = evac_eng()
                copy_op(eng, TM[:, blk, :], ps[b][:], [PS(b)], [("tm", blk)])
                out_stage(b, 512, 0, o_nav[blk // 2, i, (blk % 2) * 128:(blk % 2 + 1) * 128, :], eng)
            tok_proj(wv_v, wk_v, 512, range(4), v_block)
            exchange_recv(l, True)
            for c in (0, 1):
                for m in range(4):
                    b = std_group(wv_q, wk_q, m, c)
                    copy_op(evac_eng(), qkT[:, m, csl(c)], ps[b][:, 0:cw(c)], [PS(b)], [("qk", m, c)])
            run_attention_units(prompt_units(4, lambda jp: 4 + jp, lambda jp: slice(jp * 128, (jp + 1) * 128)))
            for g in range(4):
                load_cast(poolAs[:].rearrange("p s t -> p (s t)"), c_poolAs[g], "poolAs")
                b = nextbank()
                for s_ in range(8):
                    mm(ps[b][:, 0:256], Us[:, s_, g * 128:(g + 1) * 128], poolAs[:, s_, :], s_ == 0, s_ == 7,
                       [("Us", s_ // 2), "poolAs"], PS(b))
                mixb_tail(b, g, 1, g % 2)

            def prep(jp, hh):
                load_cast(T2s[:], nab[i, 2 * jp + hh], "T2s")
                STT(T2[:, hh, :], T2s[:], 8.0, cmask[:], ALU.mult, ALU.add, ["T2s", "cmask"], [("T2", hh)])
            run_attention_units(sample_units(4, lambda jp: jp, lambda jp: slice(jp * 128, (jp + 1) * 128),
                                             lambda jp: jp, lambda jp: slice(jp * 128, (jp + 1) * 128),
                                             lambda jp, e_, m: T2[:, e_, m * 256:(m + 1) * 256], lambda jp, e_: [("T2", e_)],
                                             prep_fn=prep))

            def src_fn(k, c):
                if k < 4:
                    return HM[:, k, csl(c)], ("hm", k, c)
                return qkT[:, 4 + k, csl(c)], ("qk", 4 + k, c)
            out_proj(l, w_out_ab[i], 16, src_fn)

        def mixer_c(l, i):
            wsrc = w_in_cx[i]

            def rope_groups(wv, wk, wp, wpk, pc):
                for m in range(4):
                    ba = std_group(wv, wk, m, 1)
                    bb = std_group(wp, wpk, m, 1)
                    t1 = next_tmp()
                    t2 = next_tmp()
                    TT(tmp[t1][:, 0:256], ps[ba][:, 0:256], ropeC[:], ALU.mult, [PS(ba), "ropeC"], [("tmp", t1)])
                    TT(tmp[t2][:, 0:256], ps[bb][:, 0:256], ropeS[:], ALU.mult, [PS(bb), "ropeS"], [("tmp", t2)])
                    if pc == 2:
                        TT(SL[:, m * 256:(m + 1) * 256], tmp[t1][:, 0:256], tmp[t2][:, 0:256], ALU.add,
                           [("tmp", t1), ("tmp", t2)], ["SLk"])
                    else:
                        TT(qkT[:, pc * 4 + m, csl(1)], tmp[t1][:, 0:256], tmp[t2][:, 0:256], ALU.add,
                           [("tmp", t1), ("tmp", t2)], [("qk", pc * 4 + m, 1)])

            def plain_groups(wv, wk, pc):
                for m in range(4):
                    b = std_group(wv, wk, m, 0)
                    copy_op(evac_eng(), qkT[:, pc * 4 + m, csl(0)], ps[b][:], [PS(b)], [("qk", pc * 4 + m, 0)])

            def kv_block(blk, b):
                eng = evac_eng()
                vsrc = ps[b][:, 256:512].rearrange("p (g c) -> p g c", g=4)
                if blk < 4:
                    for d_ in range(2):
                        copy_op(eng, TM[:, blk, :].rearrange("p (g d c) -> p g d c", g=4, d=2)[:, :, d_, :], vsrc, [PS(b)], [("tm", blk)])
                    out_stage(b, 256, 0, o_nck[blk // 2, i, (blk % 2) * 128:(blk % 2 + 1) * 128, :], eng)
                    out_stage(b, 256, 256, o_ncv[blk // 2, i, (blk % 2) * 128:(blk % 2 + 1) * 128, :], eng)
                else:
                    for d_ in range(2):
                        copy_op(eng, SL[:, 1024 + (blk - 4) * 512:1024 + (blk - 3) * 512].rearrange("p (g d c) -> p g d c", g=4, d=2)[:, :, d_, :],
                                vsrc, [PS(b)], ["SLv"])
            wv2, wk2 = piece(wsrc[:, 1024:1536], 8, 512)
            wp2, wpk2 = piece(wsrc[:, 2560:3072], 8, 512)
            rope_groups(wv2, wk2, wp2, wpk2, 2)
            wvkv, wkkv = piece(wsrc[:, 3072:3584], 8, 512)
            tok_proj(wvkv, wkkv, 512, (4, 5), kv_block)
            exchange(l, False)
            plain_groups(wv2, wk2, 2)
            qp = [(piece(wsrc[:, 0:512], 8, 512), piece(wsrc[:, 1536:2048], 8, 512))]
            tok_proj(wvkv, wkkv, 512, (0, 1, 2, 3), kv_block)
            exchange_recv(l, False)
            for pc in range(2):
                if pc == 1:
                    qp.append((piece(wsrc[:, 512:1024], 8, 512), piece(wsrc[:, 2048:2560], 8, 512)))
                (wv, wk), (wp, wpk) = qp[pc]
                rope_groups(wv, wk, wp, wpk, pc)
                plain_groups(wv, wk, pc)
            gcols = lambda jp: slice((jp // 2) * 128, (jp // 2 + 1) * 128)
            run_attention_units(
                prompt_units(8, lambda jp: 8 + jp // 2, gcols, sink_fn=lambda jp: sinkE[:, i, jp:jp + 1])
                + sample_units(8, lambda jp: jp // 2, gcols, lambda jp: jp // 2, gcols,
                               lambda jp, e_, m: wmask[:, m, :], lambda jp, e_: ["wmask"],
                               sink_fn=lambda jp: sinkE[:, i, jp:jp + 1]))
            out_proj(l, w_out_c[i], 16, lambda k, c: (HM[:, k, csl(c)], ("hm", k, c)))

        def prefetch_ctx(l):
            if l >= depth:
                return
            i = l // 2
            if l % 2 == 0:
                load_ctx(cak[i], cav[i], 512)
                load_cast(wpool[:], w_pool[i].rearrange("g c d -> c g d"), "wpool")
            else:
                load_ctx(cck[i], ccv[i], 512)

        mod_prologue()
        late_consts()
        NBLK = NT // 128
        for blk in range(NBLK):
            si = next_stg()
            c = 0 if blk < 4 else 1
            load_f32(stg[si][:], xin[blk * 128:(blk + 1) * 128, :], ("stg", si))
            for half in range(2):
                b = nextbank()
                for kk in range(4):
                    k = half * 4 + kk
                    TR(ps[b][:, kk * 128:(kk + 1) * 128], stg[si][:, k * 128:(k + 1) * 128], [("stg", si)], [PS(b)])
                copy_op(evac_eng(), xT[:, half * 4:(half + 1) * 4, blk * 128:(blk + 1) * 128],
                        ps[b][:].rearrange("p (k t) -> p k t", k=4),
                        [PS(b)], [("x", k, c) for k in range(half * 4, half * 4 + 4)])

        prefetch_ctx(0)
        mod_prologue_recv()
        for l in range(depth):
            lastl = (l == depth - 1)
            st["reserved"] = {6, 7}
            norm_phase(l, 0, 0, pre=(l > 0), order=(1, 0))
            st["reserved"] = set()
            if l % 2 == 0:
                mixer_ab(l, l // 2)
            else:
                mixer_c(l, l // 2)
            if lastl and stop == "mixer":
                break
            norm_phase(l, 1, 24, pre=True)
            st["reserved"] = set()
            prefetch_ctx(l + 1)
            ffn_phase(l)

        pre_ss = depth > 0 and stop is None
        st["reserved"] = {6, 7}
        rs = {0: rstd[:, 0:512], 1: rden[:, 0:256]}
        rkeys = {0: "rstd", 1: "rden"}
        for c in range(2):
            w = cw(c)
            if not pre_ss:
                for k in range(8):
                    ss_accum(k, c)
            ACT(rt[:, 0:w], ps[SSB[c]][:, 0:w], AF.Ln, [PS(SSB[c])], ["rt"], bias=1024 * EPS, scale=1.0)
            ACT(rs[c], rt[:, 0:w], AF.Exp, ["rt"], [rkeys[c]], scale=-0.5)
        st["reserved"] = set()
        for c in range(2):
            w = cw(c)
            ntb = w // 128
            for k in range(8):
                ti = next_tmp()
                STT(tmp[ti][:, 0:w], xT[:, k, csl(c)], gfin[:, k:k + 1], rs[c], ALU.mult, ALU.mult,
                    [("x", k, c), rkeys[c], "gfin"], [("tmp", ti)])
                for tb in range(ntb):
                    b = 2 * tb + (k // 4)
                    TR(ps[b][:, (k % 4) * 128:(k % 4 + 1) * 128], tmp[ti][:, tb * 128:(tb + 1) * 128], [("tmp", ti)], [PS(b)])
            for tb in range(ntb):
                si = next_stg()
                for hf in range(2):
                    copy_op(evac_eng(), stg[si][:, hf * 512:(hf + 1) * 512], ps[2 * tb + hf][:], [PS(2 * tb + hf)], [("stg", si)])
                if c == 0:
                    dst = o_yp[tb * 128:(tb + 1) * 128, :]
                else:
                    dst = o_ys[tb * 128:(tb + 1) * 128, :]
                DMA("sp", dst, stg[si][:], reads=[("stg", si)])

        S.emit(nc)
    return nc


def _consts_shared():
    ident = np.eye(128, dtype=np.float32)
    poolA = np.zeros((128, 4, 5, 128), np.float32)
    n = 384
    for g, w in enumerate((2, 4, 8, 16)):
        A = _pool_matrix(n, w)
        AT = A.T
        poolA[:, g, 0, :] = AT[0:128, 0:128]
        poolA[:, g, 1, :] = AT[128:256, 128:256]
        poolA[:, g, 2, :] = AT[256:384, 256:384]
        poolA[:, g, 3, :] = AT[0:128, 128:256]
        poolA[:, g, 4, :] = AT[256:384, 128:256]
    return dict(c_ident=ident, c_poolA=poolA.reshape(128, -1))


def _pool_matrix(n, w):
    A = np.zeros((n, n), np.float64)
    for tt in range(n):
        lo = min(max(tt - w // 2, 0), n)
        hi = min(max(tt - w // 2 + w, 0), n)
        A[tt, lo:hi] = 1.0 / (hi - lo)
        A[tt, tt] -= 1.0
    return A


def _consts_quarter(qi):
    cq = np.arange(64)
    qstart = np.clip(cq - 8, 0, 48)
    cp = np.arange(64)
    colvalid = (cp[:, None] >= qstart[None, :]) & (cp[:, None] < qstart[None, :] + 16)
    cm = np.full((2, 64, 8, 4, 64), NEG, np.float32)
    for rl in range(4):
        r = 4 * qi + rl
        R0 = min(max(r - 4, 0), 8)
        for m in range(8):
            for e in range(2):
                kr = 2 * m + e
                if R0 <= kr <= R0 + 7:
                    cm[e, :, m, rl, :] = np.where(colvalid, 0.0, NEG)
    cmask = cm.reshape(128, 2048)
    t = qi * 256 + np.arange(256)
    pos = np.stack([t // 64, t % 64], -1).astype(np.float32)
    inv = (10000.0 ** (-np.arange(16, dtype=np.float32) / 16)).astype(np.float32)
    ang = pos[:, :, None] * inv
    cos = np.cos(ang).astype(np.float32)
    sin = np.sin(ang).astype(np.float32)
    C = np.zeros((64, 256), np.float32)
    Sg = np.zeros((64, 256), np.float32)
    for a in range(2):
        for b in range(2):
            for f in range(16):
                d = a * 32 + b * 16 + f
                C[d] = cos[:, a, f]
                Sg[d] = -sin[:, a, f] if b == 0 else sin[:, a, f]
    ropeC = np.concatenate([C, C], 0)
    ropeS = np.concatenate([Sg, Sg], 0)
    k = np.arange(128)[:, None]
    q = np.arange(128)[None, :]
    m1 = np.where(k >= q, 0.0, NEG).astype(np.float32)
    m2 = np.where(k <= q, 0.0, NEG).astype(np.float32)
    wm = np.full((128, 8, 2, 128), NEG, np.float32)
    for qb in range(2):
        n = 2 * qi + qb
        for kn in range(8):
            if kn == n:
                wm[:, kn, qb, :] = 0.0
            elif kn == n - 1:
                wm[:, kn, qb, :] = m1
            elif kn == n + 1:
                wm[:, kn, qb, :] = m2
    wmask = wm.reshape(128, 2048)
    pA = np.zeros((4, 128, 8, 256), np.float32)
    for g, w in enumerate((2, 4, 8, 16)):
        AT = _pool_matrix(1024, w).T
        blk = AT[:, qi * 256:(qi + 1) * 256]
        pA[g] = blk.reshape(8, 128, 256).transpose(1, 0, 2)
    return dict(c_cmask=cmask, c_ropeC=ropeC, c_ropeS=ropeS, c_wmask=wmask, c_poolAs=pA.reshape(4, 128, 2048))


def _fm(v):
    v = np.asarray(v, np.float32)
    lead = v.shape[:-1]
    m = v.shape[-1] // 128
    return np.ascontiguousarray(np.moveaxis(v.reshape(lead + (m, 128)), -1, 0))


_NC_CACHE = {}
_DEBUG = {}


def make_in_maps(x_prompt, x_sample, cache_a_k, cache_a_v, cache_c_k, cache_c_v, c, c_ctx,
                 w_mod, b_mod, norm_mix, norm_ffn, w_in_ab, rpb_a, w_pool, pool_scale, w_out_ab,
                 w_in_c, sink_c, w_out_c, w_gate_up, w_down, norm_final):
    f = lambda a: np.ascontiguousarray(np.asarray(a, dtype=np.float32))
    x_prompt, x_sample = f(x_prompt), f(x_sample)
    cache_a_k, cache_a_v, cache_c_k, cache_c_v = f(cache_a_k), f(cache_a_v), f(cache_c_k), f(cache_c_v)
    c, c_ctx = f(c), f(c_ctx)
    w_in_c = f(w_in_c)
    rpb_a = f(rpb_a)
    sink_c = f(sink_c)

    d = np.arange(64)
    partner = (d // 32) * 32 + (1 - (d % 32) // 16) * 16 + (d % 16)
    qcols = np.arange(1024)
    qperm = (qcols // 64) * 64 + partner[qcols % 64]
    kdup = np.concatenate([1024 + g * 64 + d for g in range(4) for _ in range(2)])
    kdup_perm = np.concatenate([1024 + g * 64 + partner for g in range(4) for _ in range(2)])
    kv = np.arange(1024, 1536)
    colidx = np.concatenate([qcols, kdup, qperm, kdup_perm, kv])
    w_in_cx = np.ascontiguousarray(w_in_c[:, :, colidx])

    cp = np.arange(64)[:, None]
    cq = np.arange(64)[None, :]
    dcol = np.clip(cp - cq + 15, 0, 30)
    nabs = []
    for qi in range(4):
        t_ = np.zeros((2, 8, 2, 64, 8, 4, 64), np.float32)
        for e in range(2):
            for m in range(8):
                for rl in range(4):
                    row = min(max(2 * m + e - (4 * qi + rl) + 7, 0), 14)
                    t_[:, :, e, :, m, rl, :] = rpb_a[:, :, row][:, :, dcol]
        nabs.append(t_.reshape(2, 8, 128, 2048))

    shared = dict(
        gmixT=_fm(norm_mix).reshape(128, -1),
        gffnT=_fm(norm_ffn).reshape(128, -1), gfinT=_fm(norm_final).reshape(128, -1),
        w_in_ab=f(w_in_ab), w_out_ab=f(w_out_ab), w_pool=f(w_pool), pscT=_fm(pool_scale).reshape(128, -1),
        w_in_cx=w_in_cx, w_out_c=f(w_out_c),
        sinkT=np.ascontiguousarray(np.stack([sink_c[:, 0::2], sink_c[:, 1::2]], 0).repeat(64, axis=0).reshape(128, 16)),
        w_gu=f(w_gate_up), w_dn=f(w_down), **_consts_shared())
    qconst = [_consts_quarter(qi) for qi in range(4)]
    w_mod = f(w_mod)
    b_mod = f(b_mod)
    wmq = [np.ascontiguousarray(w_mod[:, :, qi * 1536:(qi + 1) * 1536]) for qi in range(4)]
    bmq = [_fm(b_mod[:, qi * 1536:(qi + 1) * 1536]).reshape(128, 48) for qi in range(4)]
    in_maps = []
    for core in range(8):
        b = core // 4
        qi = core % 4
        xin = np.concatenate([x_prompt[2 * core].reshape(256, 1024), x_prompt[2 * core + 1].reshape(256, 1024),
                              x_sample[b, qi * 256:(qi + 1) * 256]], 0)
        cvec = np.stack([c_ctx, c[b]], 0)
        cvT = np.ascontiguousarray(_fm(cvec).transpose(0, 2, 1)).reshape(128, 16)
        cck = np.ascontiguousarray(cache_c_k[b].reshape(2, 512, 4, 1, 64).repeat(2, axis=3).reshape(2, 512, 512))
        m = dict(shared)
        m.update(qconst[qi])
        m.update(xin=np.ascontiguousarray(xin), cvT=cvT, nab=nabs[qi], w_mod=wmq[qi], bmodT=bmq[qi],
                 cak=np.ascontiguousarray(cache_a_k[b].reshape(2, 512, 512)),
                 cav=np.ascontiguousarray(cache_a_v[b].reshape(2, 512, 512)),
                 cck=cck, ccv=np.ascontiguousarray(cache_c_v[b].reshape(2, 512, 4, 1, 64).repeat(2, axis=3).reshape(2, 512, 512)))
        in_maps.append(m)
    return in_maps


def kernel(**inputs):
    if "nc" not in _NC_CACHE:
        _NC_CACHE["nc"] = build_program(*_DEBUG.get("args", ()))
    nc = _NC_CACHE["nc"]
    in_maps = make_in_maps(**inputs)
    res = run_bass_kernel_spmd(nc, in_maps, core_ids=list(range(8)))
    R = res.results
    y_prompt = np.concatenate([R[i]["o_yp"].reshape(2, 256, 1024) for i in range(8)], 0)
    y_sample = np.concatenate([R[i]["o_ys"] for i in range(8)], 0).reshape(2, 1024, 1024)
    nak = np.concatenate([R[i]["o_nak"] for i in range(8)], 0).reshape(16, 2, 256, 8, 64)
    nav = np.concatenate([R[i]["o_nav"] for i in range(8)], 0).reshape(16, 2, 256, 8, 64)
    nck = np.concatenate([R[i]["o_nck"] for i in range(8)], 0).reshape(16, 2, 256, 4, 64)
    ncv = np.concatenate([R[i]["o_ncv"] for i in range(8)], 0).reshape(16, 2, 256, 4, 64)
    return (y_prompt.astype(np.float32), y_sample.astype(np.float32), nak.astype(np.float32),
            nav.astype(np.float32), nck.astype(np.float32), ncv.astype(np.float32))
```
